# Optimizing a Trainium2 kernel written in Bass

```python
import jax, jax.numpy as jnp
from jax import lax
import numpy as np

D_MODEL = 1024
BATCH = 32
SEQ = 2048
DEPTH = 2
DEC_BATCH = 8
DEC_SEQ = 32
PAST_LEN = 1024

CHUNK = 64
HEAD_DIM = 64
N_HEADS = D_MODEL // HEAD_DIM
HA = N_HEADS // 2
HB = N_HEADS - HA
HC = N_HEADS
A_PAST_CHUNKS = 8
A_PAST = A_PAST_CHUNKS * CHUNK
A_BAND = A_PAST + CHUNK
REL_CLIP = 128
N_REL = 2 * REL_CLIP + 1
Q_BLOCK = 128
N_MEM = 256
XA_HEADS = 4
XA_HEAD_DIM = D_MODEL // XA_HEADS
D_FF = -(-8 * D_MODEL // (3 * 256)) * 256
N_EVEN = (DEPTH + 1) // 2
N_ODD = DEPTH // 2
FORGET_BIAS = 2.0
RMS_EPS = 1e-6
NEG_INF = -1e30

kernel_name = 'hybrid_stream_encoder_step'


def rmsnorm(x, g):
    xf = x.astype(jnp.float32)
    y = xf * lax.rsqrt(jnp.mean(xf * xf, axis=-1, keepdims=True) + RMS_EPS)
    return (y * g.astype(jnp.float32)).astype(x.dtype)


def to_blocks(x, size):
    b, s = x.shape[:2]
    return jnp.moveaxis(x.reshape((b, s // size, size) + x.shape[2:]), 1, 0)


def from_blocks(x):
    n, b, size = x.shape[:3]
    return jnp.moveaxis(x, 0, 1).reshape((b, n * size) + x.shape[3:])


def merge_heads(*outs):
    o = jnp.concatenate(outs, axis=2)
    return o.reshape(o.shape[0], o.shape[1], -1)


def rel_bias_table(rel_bias, q_loc, k_loc):
    rel = jnp.clip(q_loc[:, None] - k_loc[None, :], -REL_CLIP, REL_CLIP) + REL_CLIP
    return jnp.moveaxis(rel_bias[rel].astype(jnp.float32), -1, 0)


def band_attend(q, k, v, bias, valid=None):
    s = jnp.einsum('bqhd,bkhd->bhqk', q, k).astype(jnp.float32) * (HEAD_DIM ** -0.5) + bias
    if valid is not None:
        s = jnp.where(valid, s, NEG_INF)
    p = jax.nn.softmax(s, axis=-1)
    return jnp.einsum('bhqk,bkhd->bqhd', p.astype(v.dtype), v)


def chunk_band_attn_prompt(q, k, v, rel_bias):
    s_len = q.shape[1]
    pad = ((0, 0), (A_PAST, 0), (0, 0), (0, 0))
    kp, vp = jnp.pad(k, pad), jnp.pad(v, pad)
    k_loc = jnp.arange(A_BAND)
    bias = rel_bias_table(rel_bias, A_PAST + jnp.arange(CHUNK), k_loc)

    def one_chunk(args):
        c, qc = args
        start = c * CHUNK
        kb = lax.dynamic_slice_in_dim(kp, start, A_BAND, axis=1)
        vb = lax.dynamic_slice_in_dim(vp, start, A_BAND, axis=1)
        valid = (start - A_PAST + k_loc) >= 0
        return band_attend(qc, kb, vb, bias, valid)

    out = lax.map(one_chunk, (jnp.arange(s_len // CHUNK), to_blocks(q, CHUNK)))
    return from_blocks(out)


def chunk_band_attn_sample(q, k_new, v_new, k_cache, v_cache, rel_bias):
    n_past, t = k_cache.shape[1], q.shape[1]
    k = jnp.concatenate([k_cache, k_new], axis=1)
    v = jnp.concatenate([v_cache, v_new], axis=1)
    bias = rel_bias_table(rel_bias, n_past + jnp.arange(t), jnp.arange(n_past + t))
    return band_attend(q, k, v, bias)


def stick_break_attend(q, k, v, q_pos, k_pos):
    z = jnp.einsum('bqhd,bkhd->bhqk', q, k).astype(jnp.float32) * (HEAD_DIM ** -0.5)
    before = k_pos[None, :] < q_pos[:, None]
    log_keep = jnp.where(before, jax.nn.log_sigmoid(-z), 0.0)
    log_later = lax.cumsum(log_keep, axis=3, reverse=True) - log_keep
    w = jnp.where(before, jnp.exp(jax.nn.log_sigmoid(z) + log_later), 0.0)
    return jnp.einsum('bhqk,bkhd->bqhd', w.astype(v.dtype), v)


def stick_break_prompt(q, k, v):
    s_len = q.shape[1]
    k_pos = jnp.arange(s_len)

    def one_block(args):
        i, qb = args
        return stick_break_attend(qb, k, v, i * Q_BLOCK + jnp.arange(Q_BLOCK), k_pos)

    return from_blocks(lax.map(one_block, (jnp.arange(s_len // Q_BLOCK), to_blocks(q, Q_BLOCK))))


def stick_break_sample(q, k_new, v_new, k_cache, v_cache):
    n_past, t = k_cache.shape[1], q.shape[1]
    k = jnp.concatenate([k_cache, k_new], axis=1)
    v = jnp.concatenate([v_cache, v_new], axis=1)
    return stick_break_attend(q, k, v, n_past + jnp.arange(t), jnp.arange(n_past + t))


def forget_attend(q, k, v, fq, fk, q_pos, k_pos):
    decay = jnp.swapaxes(fq, 1, 2)[..., :, None] - jnp.swapaxes(fk, 1, 2)[..., None, :]
    s = jnp.einsum('bqhd,bkhd->bhqk', q, k).astype(jnp.float32) * (HEAD_DIM ** -0.5) + decay
    s = jnp.where(k_pos[None, :] <= q_pos[:, None], s, NEG_INF)
    p = jax.nn.softmax(s, axis=-1)
    return jnp.einsum('bhqk,bkhd->bqhd', p.astype(v.dtype), v)


def forget_attn_prompt(q, k, v, log_f):
    s_len = q.shape[1]
    cum = jnp.cumsum(log_f.astype(jnp.float32), axis=1)
    k_pos = jnp.arange(s_len)

    def one_block(args):
        i, qb, fb = args
        return forget_attend(qb, k, v, fb, cum, i * Q_BLOCK + jnp.arange(Q_BLOCK), k_pos)

    blocks = (jnp.arange(s_len // Q_BLOCK), to_blocks(q, Q_BLOCK), to_blocks(cum, Q_BLOCK))
    return from_blocks(lax.map(one_block, blocks))


def forget_attn_sample(q, k_new, v_new, lf_new, k_cache, v_cache, lf_cache):
    n_past, t = k_cache.shape[1], q.shape[1]
    k = jnp.concatenate([k_cache, k_new], axis=1)
    v = jnp.concatenate([v_cache, v_new], axis=1)
    lf = jnp.concatenate([lf_cache.astype(jnp.float32), lf_new.astype(jnp.float32)], axis=1)
    cum = jnp.cumsum(lf, axis=1)
    return forget_attend(q, k, v, cum[:, n_past:], cum, n_past + jnp.arange(t), jnp.arange(n_past + t))


def proj_ab(h, w_in):
    b, t, _ = h.shape
    wa, wb = HA * HEAD_DIM, HB * HEAD_DIM
    cuts = [int(c) for c in np.cumsum([wa, wa, wa, wb, wb])]
    parts = jnp.split(h @ w_in, cuts, axis=-1)
    heads = [HA, HA, HA, HB, HB, HB]
    return [p.reshape(b, t, n, HEAD_DIM) for p, n in zip(parts, heads)]


def proj_c(h, w_in, b_f):
    b, t, _ = h.shape
    w = HC * HEAD_DIM
    z = h @ w_in
    q = z[..., :w].reshape(b, t, HC, HEAD_DIM)
    k = z[..., w:2 * w].reshape(b, t, HC, HEAD_DIM)
    v = z[..., 2 * w:3 * w].reshape(b, t, HC, HEAD_DIM)
    log_f = jax.nn.log_sigmoid(z[..., 3 * w:].astype(jnp.float32) + b_f.astype(jnp.float32))
    return q, k, v, log_f


def memory_kv(mem, g, w_k, w_v):
    b = mem.shape[0]
    m = rmsnorm(mem, g)
    mk = (m @ w_k).reshape(b, N_MEM, XA_HEADS, XA_HEAD_DIM)
    mv = (m @ w_v).reshape(b, N_MEM, XA_HEADS, XA_HEAD_DIM)
    return mk, mv


def memory_attn(h, mk, mv, w_q, w_o):
    b, t, _ = h.shape
    q = (h @ w_q).reshape(b, t, XA_HEADS, XA_HEAD_DIM)
    s = jnp.einsum('bqhd,bkhd->bhqk', q, mk).astype(jnp.float32) * (XA_HEAD_DIM ** -0.5)
    p = jax.nn.softmax(s, axis=-1)
    o = jnp.einsum('bhqk,bkhd->bqhd', p.astype(mv.dtype), mv)
    return o.reshape(b, t, D_MODEL) @ w_o


def swiglu(h, w_gate, w_up, w_down):
    return (jax.nn.silu(h @ w_gate) * (h @ w_up)) @ w_down


def setup_inputs(seed: int = 0) -> dict:
    key = jax.random.key(seed)
    ks = iter(jax.random.split(key, 32))

    def nrm(shape, scale=1.0):
        return scale * jax.random.normal(next(ks), shape, jnp.float32)

    a_rows = min(A_PAST, PAST_LEN)
    d_in = D_MODEL ** -0.5
    inp = {}
    inp['x_prompt'] = nrm((BATCH, SEQ, D_MODEL))
    inp['x_sample'] = nrm((DEC_BATCH, DEC_SEQ, D_MODEL))
    inp['cache_a_k'] = nrm((N_EVEN, DEC_BATCH, a_rows, HA, HEAD_DIM))
    inp['cache_a_v'] = nrm((N_EVEN, DEC_BATCH, a_rows, HA, HEAD_DIM))
    inp['cache_b_k'] = nrm((N_EVEN, DEC_BATCH, PAST_LEN, HB, HEAD_DIM))
    inp['cache_b_v'] = nrm((N_EVEN, DEC_BATCH, PAST_LEN, HB, HEAD_DIM))
    inp['cache_c_k'] = nrm((N_ODD, DEC_BATCH, PAST_LEN, HC, HEAD_DIM))
    inp['cache_c_v'] = nrm((N_ODD, DEC_BATCH, PAST_LEN, HC, HEAD_DIM))
    inp['cache_c_logf'] = jax.nn.log_sigmoid(FORGET_BIAS + nrm((N_ODD, DEC_BATCH, PAST_LEN, HC)))
    inp['cache_mem_k'] = nrm((DEPTH, DEC_BATCH, N_MEM, XA_HEADS, XA_HEAD_DIM))
    inp['cache_mem_v'] = nrm((DEPTH, DEC_BATCH, N_MEM, XA_HEADS, XA_HEAD_DIM))
    inp['mem_prompt'] = nrm((BATCH, N_MEM, D_MODEL))
    inp['w_in_ab'] = nrm((N_EVEN, D_MODEL, 3 * (HA + HB) * HEAD_DIM), d_in)
    inp['w_out_ab'] = nrm((N_EVEN, (HA + HB) * HEAD_DIM, D_MODEL), ((HA + HB) * HEAD_DIM) ** -0.5)
    inp['rel_bias_a'] = nrm((N_EVEN, N_REL, HA), 0.1)
    inp['w_in_c'] = nrm((N_ODD, D_MODEL, 3 * HC * HEAD_DIM + HC), d_in)
    inp['b_f_c'] = FORGET_BIAS + nrm((N_ODD, HC), 0.1)
    inp['w_out_c'] = nrm((N_ODD, HC * HEAD_DIM, D_MODEL), (HC * HEAD_DIM) ** -0.5)
    inp['g_mix'] = 1.0 + nrm((DEPTH, D_MODEL), 0.05)
    inp['g_xattn'] = 1.0 + nrm((DEPTH, D_MODEL), 0.05)
    inp['g_mem'] = 1.0 + nrm((DEPTH, D_MODEL), 0.05)
    inp['w_xq'] = nrm((DEPTH, D_MODEL, D_MODEL), d_in)
    inp['w_xk'] = nrm((DEPTH, D_MODEL, D_MODEL), d_in)
    inp['w_xv'] = nrm((DEPTH, D_MODEL, D_MODEL), d_in)
    inp['w_xo'] = nrm((DEPTH, D_MODEL, D_MODEL), d_in)
    inp['g_ffn'] = 1.0 + nrm((DEPTH, D_MODEL), 0.05)
    inp['w_gate'] = nrm((DEPTH, D_MODEL, D_FF), d_in)
    inp['w_up'] = nrm((DEPTH, D_MODEL, D_FF), d_in)
    inp['w_down'] = nrm((DEPTH, D_FF, D_MODEL), D_FF ** -0.5)
    inp['g_final'] = 1.0 + nrm((D_MODEL,), 0.05)
    return inp


def reference(x_prompt, x_sample, cache_a_k, cache_a_v, cache_b_k, cache_b_v, cache_c_k, cache_c_v,
              cache_c_logf, cache_mem_k, cache_mem_v, mem_prompt, w_in_ab, w_out_ab, rel_bias_a,
              w_in_c, b_f_c, w_out_c, g_mix, g_xattn, g_mem, w_xq, w_xk, w_xv, w_xo, g_ffn,
              w_gate, w_up, w_down, g_final):
    xp, xs = x_prompt, x_sample
    a_kp, a_vp, b_kp, b_vp, a_ks, a_vs, b_ks, b_vs = [], [], [], [], [], [], [], []
    c_kp, c_vp, c_lfp, c_ks, c_vs, c_lfs = [], [], [], [], [], []
    mem_kp, mem_vp = [], []
    for layer in range(DEPTH):
        if layer % 2 == 0:
            e = layer // 2
            qa, ka, va, qb, kb, vb = proj_ab(rmsnorm(xp, g_mix[layer]), w_in_ab[e])
            o = merge_heads(chunk_band_attn_prompt(qa, ka, va, rel_bias_a[e]),
                            stick_break_prompt(qb, kb, vb))
            xp = xp + o @ w_out_ab[e]
            keep = min(A_PAST, xp.shape[1])
            a_kp.append(ka[:, -keep:]); a_vp.append(va[:, -keep:])
            b_kp.append(kb); b_vp.append(vb)
            qa, ka, va, qb, kb, vb = proj_ab(rmsnorm(xs, g_mix[layer]), w_in_ab[e])
            o = merge_heads(chunk_band_attn_sample(qa, ka, va, cache_a_k[e], cache_a_v[e], rel_bias_a[e]),
                            stick_break_sample(qb, kb, vb, cache_b_k[e], cache_b_v[e]))
            xs = xs + o @ w_out_ab[e]
            a_ks.append(ka); a_vs.append(va); b_ks.append(kb); b_vs.append(vb)
        else:
            c = layer // 2
            q, k, v, lf = proj_c(rmsnorm(xp, g_mix[layer]), w_in_c[c], b_f_c[c])
            xp = xp + merge_heads(forget_attn_prompt(q, k, v, lf)) @ w_out_c[c]
            c_kp.append(k); c_vp.append(v); c_lfp.append(lf)
            q, k, v, lf = proj_c(rmsnorm(xs, g_mix[layer]), w_in_c[c], b_f_c[c])
            o = forget_attn_sample(q, k, v, lf, cache_c_k[c], cache_c_v[c], cache_c_logf[c])
            xs = xs + merge_heads(o) @ w_out_c[c]
            c_ks.append(k); c_vs.append(v); c_lfs.append(lf)
        mk, mv = memory_kv(mem_prompt, g_mem[layer], w_xk[layer], w_xv[layer])
        mem_kp.append(mk); mem_vp.append(mv)
        xp = xp + memory_attn(rmsnorm(xp, g_xattn[layer]), mk, mv, w_xq[layer], w_xo[layer])
        xs = xs + memory_attn(rmsnorm(xs, g_xattn[layer]), cache_mem_k[layer], cache_mem_v[layer],
                              w_xq[layer], w_xo[layer])
        xp = xp + swiglu(rmsnorm(xp, g_ffn[layer]), w_gate[layer], w_up[layer], w_down[layer])
        xs = xs + swiglu(rmsnorm(xs, g_ffn[layer]), w_gate[layer], w_up[layer], w_down[layer])
    y_prompt = rmsnorm(xp, g_final)
    y_sample = rmsnorm(xs, g_final)
    return (y_prompt, y_sample,
            jnp.stack(a_kp), jnp.stack(a_vp), jnp.stack(b_kp), jnp.stack(b_vp),
            jnp.stack(c_kp), jnp.stack(c_vp), jnp.stack(c_lfp),
            jnp.stack(mem_kp), jnp.stack(mem_vp),
            jnp.stack(a_ks), jnp.stack(a_vs), jnp.stack(b_ks), jnp.stack(b_vs),
            jnp.stack(c_ks), jnp.stack(c_vs), jnp.stack(c_lfs))
```

```python
import numpy as np
from contextlib import ExitStack
import concourse.bass as bass
import concourse.mybir as mybir
from concourse.bass_utils import run_bass_kernel_spmd

F32 = mybir.dt.float32
BF16 = mybir.dt.bfloat16
AF = mybir.ActivationFunctionType
ALU = mybir.AluOpType

NSLOT = 8
D = 1024
NCH = 8
DFF = 2816
NFC = 22
NMEM = 256
EPS = 1e-6
LEXT = 768
MASKV = -80000.0


class Prog:
    ENG = ['pe', 'act', 'dve', 'pool', 'sp']

    def __init__(self):
        self.ops = {e: [] for e in self.ENG}
        self.lastw = {}
        self.readers = {}
        self.ndma = {'sp': 0, 'pool': 0, 'act': 0}
        self.bar = {}
        self.bard = set()

    def add(self, eng, fn, r=(), w=(), dma=False, nobar=False):
        deps = dict(self.bar)
        ddeps = set(self.bard)

        def dep(p):
            e, i = p
            if self.ops[e][i]['dma']:
                ddeps.add(p)
            elif deps.get(e, -1) < i:
                deps[e] = i
        for b in r:
            if b in self.lastw:
                dep(self.lastw[b])
        for b in w:
            if b in self.lastw:
                dep(self.lastw[b])
            for p in self.readers.get(b, ()):
                dep(p)
        idx = len(self.ops[eng])
        if eng == 'pe':
            if 'pe' in self.bar:
                deps['pe'] = self.bar['pe']
            else:
                deps.pop('pe', None)
        op = dict(fn=fn, deps=deps, ddeps=ddeps, dma=dma, sig=False, nobar=nobar)
        if dma:
            k = self.ndma[eng]
            self.ndma[eng] = k + 1
            op['slot'] = k % NSLOT
            op['val'] = 16 * (k // NSLOT + 1)
        self.ops[eng].append(op)
        for b in r:
            self.readers.setdefault(b, []).append((eng, idx))
        for b in w:
            self.lastw[b] = (eng, idx)
            self.readers[b] = []
        return (eng, idx)

    def barrier(self):
        self.bar = {e: len(self.ops[e]) - 1 for e in self.ENG if len(self.ops[e]) > 0
                    and not all(o['dma'] for o in self.ops[e])}
        for e in list(self.bar):
            i = self.bar[e]
            while i >= 0 and self.ops[e][i]['dma']:
                i -= 1
            if i < 0:
                del self.bar[e]
            else:
                self.bar[e] = i
        self.bard = set()
        for q in ('sp', 'pool', 'act'):
            cnt = 0
            for i in range(len(self.ops[q]) - 1, -1, -1):
                if self.ops[q][i]['dma'] and not self.ops[q][i]['nobar']:
                    self.bard.add((q, i))
                    cnt += 1
                    if cnt >= NSLOT:
                        break

    def finish(self):
        tails = []
        for q in ('sp', 'pool', 'act'):
            cnt = 0
            for i in range(len(self.ops[q]) - 1, -1, -1):
                if self.ops[q][i]['dma']:
                    tails.append((q, i))
                    cnt += 1
                    if cnt >= NSLOT:
                        break
        op = dict(fn=lambda eng: eng.nop(), deps={}, ddeps=set(tails), dma=False, sig=False)
        self.ops['sp'].append(op)

    def emit(self, nc, ctx):
        ops = self.ops
        for e in self.ENG:
            for op in ops[e]:
                for pe_, pi in op['deps'].items():
                    ops[pe_][pi]['sig'] = True
        for e in self.ENG:
            c = 0
            for op in ops[e]:
                if op['dma']:
                    continue
                if op['sig']:
                    c += 1
                    op['val'] = c
        sems = {e: ctx.enter_context(nc.semaphore(f"s_{e}")) for e in self.ENG}
        dsems = {q: [ctx.enter_context(nc.semaphore(f"d_{q}{i}")) for i in range(NSLOT)]
                 for q in ('sp', 'pool', 'act') if self.ndma[q] > 0}
        block = ctx.enter_context(nc.Block())

        def run(e, eng):
            seen = {}
            dseen = {}
            for op in ops[e]:
                for pe_, pi in op['deps'].items():
                    v = ops[pe_][pi]['val']
                    if seen.get(pe_, 0) < v:
                        eng.wait_ge(sems[pe_], v)
                        seen[pe_] = v
                for (q, qi) in sorted(op['ddeps']):
                    d = ops[q][qi]
                    key = (q, d['slot'])
                    if dseen.get(key, 0) < d['val']:
                        eng.wait_ge(dsems[q][d['slot']], d['val'])
                        dseen[key] = d['val']
                if op['dma']:
                    key = (e, op['slot'])
                    pv = op['val'] - 16
                    if pv > 0 and dseen.get(key, 0) < pv:
                        eng.wait_ge(dsems[e][op['slot']], pv)
                        dseen[key] = pv
                    op['fn'](eng).then_inc(dsems[e][op['slot']], 16)
                else:
                    ins = op['fn'](eng)
                    if op['sig']:
                        ins.then_inc(sems[e], 1)

        block.tensor(lambda eng: run('pe', eng))
        block.scalar(lambda eng: run('act', eng))
        block.vector(lambda eng: run('dve', eng))
        block.gpsimd(lambda eng: run('pool', eng))
        block.sync(lambda eng: run('sp', eng))


class StopBuild(Exception):
    pass


class Builder:
    def ck(self, n):
        if self.cfg.get('stop', 10 ** 9) <= n:
            raise StopBuild()

    def __init__(self, cfg):
        self.cfg = cfg
        self.NS = cfg['NS']
        self.T = cfg['T']
        self.TS = cfg['TS']
        self.PA, self.PB, self.PC = cfg['PA'], cfg['PB'], cfg['PC']
        self.KEEP = min(512, self.T)
        self.nc = bass.Bass("TRN2", target_bir_lowering=False)
        self.P = Prog()
        self.ctx = ExitStack()
        self.rp = {}
        self.uid = 0

    def dram_in(self, name, shape):
        return self.nc.dram_tensor(name, list(shape), F32, kind="ExternalInput").ap()

    def dram_out(self, name, shape):
        return self.nc.dram_tensor(name, list(shape), F32, kind="ExternalOutput").ap()

    def sb(self, name, shape, dt):
        return self.ctx.enter_context(self.nc.sbuf_tensor(name, list(shape), dt))

    def bank(self, grp):
        ring = self.rings[grp]
        i = ring[self.rp.get(grp, 0) % len(ring)]
        self.rp[grp] = self.rp.get(grp, 0) + 1
        return i

    def mm(self, out, lhsT, rhs, st, sp, r, w):
        self.P.add('pe', lambda e: e.matmul(out, lhsT=lhsT, rhs=rhs, start=st, stop=sp), r=r, w=w)

    def tr(self, out, in_, ident, r, w):
        self.P.add('pe', lambda e: e.transpose(out, in_, ident), r=r, w=w)

    def act(self, out, in_, func, r, w, bias=0.0, scale=1.0, accum=None):
        if accum is None:
            self.P.add('act', lambda e: e.activation(out=out, in_=in_, func=func, bias=bias, scale=scale), r=r, w=w)
        else:
            self.P.add('act', lambda e: e.activation(out=out, in_=in_, func=func, bias=bias, scale=scale,
                                                     accum_out=accum), r=r, w=w)

    def copy(self, eng, out, in_, r, w):
        if eng == 'act':
            self.P.add('act', lambda e: e.copy(out=out, in_=in_), r=r, w=w)
        else:
            self.P.add(eng, lambda e: e.tensor_copy(out=out, in_=in_), r=r, w=w)

    def tt(self, eng, out, in0, in1, op, r, w):
        self.P.add(eng, lambda e: e.tensor_tensor(out=out, in0=in0, in1=in1, op=op), r=r, w=w)

    def ts(self, eng, out, in0, s1, s2, op0, op1, r, w):
        if op1 is None:
            self.P.add(eng, lambda e: e.tensor_scalar(out=out, in0=in0, scalar1=s1, scalar2=None, op0=op0), r=r, w=w)
        else:
            self.P.add(eng, lambda e: e.tensor_scalar(out=out, in0=in0, scalar1=s1, scalar2=s2, op0=op0, op1=op1),
                       r=r, w=w)

    def stt(self, eng, out, in0, scalar, in1, op0, op1, r, w):
        self.P.add(eng, lambda e: e.scalar_tensor_tensor(out=out, in0=in0, scalar=scalar, in1=in1, op0=op0, op1=op1),
                   r=r, w=w)

    def dma(self, q, out, in_, r, w, nobar=False):
        self.P.add(q, lambda e: e.dma_start(out=out, in_=in_), r=r, w=w, dma=True, nobar=nobar)

    def memset(self, eng, ap, val, w):
        self.P.add(eng, lambda e: e.memset(ap, val), r=(), w=w)

    def declare(self):
        NS, T, TS = self.NS, self.T, self.TS
        PA, PB, PC, KEEP = self.PA, self.PB, self.PC, self.KEEP
        di, do = self.dram_in, self.dram_out
        self.xp = di("xp", [NS, T, D])
        self.xs = di("xs", [TS, D])
        self.mem = di("mem", [NS, NMEM, D])
        self.ca_k = di("ca_k", [PA, 512]); self.ca_v = di("ca_v", [PA, 512])
        self.cb_k = di("cb_k", [PB, 512]); self.cb_v = di("cb_v", [PB, 512])
        self.cc_k = di("cc_k", [PC, 1024]); self.cc_v = di("cc_v", [PC, 1024])
        self.cc_lf = di("cc_lf", [PC, 16])
        self.cm_k = di("cm_k", [2, NMEM, D]); self.cm_v = di("cm_v", [2, NMEM, D])
        self.w_in_ab = di("w_in_ab", [D, 3072]); self.w_out_ab = di("w_out_ab", [D, D])
        self.relext = di("relext", [8, LEXT])
        self.w_in_c = di("w_in_c", [D, 3088]); self.b_f = di("b_f", [16]); self.w_out_c = di("w_out_c", [D, D])
        self.gvec = di("gvec", [128, 72])
        self.w_xq = di("w_xq", [2, D, D]); self.w_xk = di("w_xk", [2, D, D])
        self.w_xv = di("w_xv", [2, D, D]); self.w_xo = di("w_xo", [2, D, D])
        self.w_gate = di("w_gate", [2, D, DFF]); self.w_up = di("w_up", [2, D, DFF])
        self.w_down = di("w_down", [2, DFF, D])
        self.cf32 = di("cf32", [128, 384])
        self.cb16 = di("cb16", [128, 1410 + 896])
        self.csel = di("csel", [128, 16 * 128])
        self.y_p = do("y_p", [NS, T, D]); self.y_s = do("y_s", [TS, D])
        self.a_k_p = do("a_k_p", [NS, KEEP, 512]); self.a_v_p = do("a_v_p", [NS, KEEP, 512])
        self.b_k_p = do("b_k_p", [NS, T, 512]); self.b_v_p = do("b_v_p", [NS, T, 512])
        self.c_k_p = do("c_k_p", [NS, T, 1024]); self.c_v_p = do("c_v_p", [NS, T, 1024])
        self.c_lf_p = do("c_lf_p", [NS, T, 16])
        self.mem_k_p = do("mem_k_p", [2, NS, NMEM, D]); self.mem_v_p = do("mem_v_p", [2, NS, NMEM, D])
        self.a_k_s = do("a_k_s", [TS, 512]); self.a_v_s = do("a_v_s", [TS, 512])
        self.b_k_s = do("b_k_s", [TS, 512]); self.b_v_s = do("b_v_s", [TS, 512])
        self.c_k_s = do("c_k_s", [TS, 1024]); self.c_v_s = do("c_v_s", [TS, 1024])
        self.c_lf_s = do("c_lf_s", [TS, 16])

        sb = self.sb
        TM = max(T, TS)
        self.TM = TM
        NKT = max(T // 128, max(PB, PC) // 128 + 1, 2)
        self.NKT = NKT
        KTC = max(T, max(PB, PC) + TS)
        self.xT = sb("xT", [128, NCH, TM], F32)
        self.hT = sb("hT", [128, NCH, TM], BF16)
        self.oT = sb("oT", [128, NCH, TM], BF16)
        QN, KN, VN = TM, KTC, NKT * 192
        XT = min(512, TM)
        RN = max(QN + KN + VN, 2 * XT + 3 * 2048)
        self.R = sb("R", [128, RN], BF16)
        self.qT = self.R[:, 0:QN]
        self.kT = self.R[:, QN:QN + KN]
        self.vS = self.R[:, QN + KN:QN + KN + VN].rearrange("p (k n) -> p k n", n=192)
        self.q2 = self.R[:, 0:2 * XT].rearrange("p (a t) -> p a t", a=2)
        self.mkT = self.R[:, 2 * XT:2 * XT + 2048].rearrange("p (c m) -> p c m", c=NCH)
        self.mv = self.R[:, 2 * XT + 2048:2 * XT + 4096].rearrange("p (a n) -> p a n", a=2)
        self.mhT = self.R[:, 2 * XT + 4096:2 * XT + 6144].rearrange("p (c m) -> p c m", c=NCH)
        self.wsl = [sb(f"wsl{i}", [128, 4096], BF16) for i in range(2)]
        self.wrp = 0
        self.biasA = sb("biasA", [128, max(8 * 4 * 128, TM)], BF16)
        self.cumT8 = self.biasA[:, 0:TM]
        self.cB = sb("cB", [128, 1410 + 896], BF16)
        self.cF = sb("cF", [128, 384], F32)
        self.sel = sb("sel", [128, 16 * 128], BF16)
        self.gv = sb("gv", [128, 72], F32)
        self.bfb = sb("bfb", [128, 16], F32)
        self.xst = sb("xst", [128, D], F32)
        self.toe = self.xst[:, :].rearrange("p (h q) -> p h q", h=8)
        self.kvst = [sb(f"kvst{i}", [128, 384], F32) for i in range(2)]
        self.kvrp = 0
        self.tmpf = [sb(f"tmpf{i}", [128, 512], F32) for i in range(2)]
        self.spb = [sb(f"spb{i}", [128, 512], BF16) for i in range(4)]
        self.sprp = 0
        self.tfrp = 0
        self.wtb = [sb(f"wtb{i}", [128, 512], BF16) for i in range(4)]
        self.wtrp = 0
        self.racc = [sb(f"racc{i}", [128, 512], BF16) for i in range(4)]
        self.sqr = [sb(f"sq{i}", [128, 512], BF16) for i in range(2)]
        self.rden = sb("rden", [128, 512], F32)
        self.rstd = self.rden
        self.negcum = sb("negcum", [128, NKT, 16], F32)
        self.lfs = sb("lfs", [128, NKT, 16], F32)
        self.sm = sb("sm", [128, 160], F32)
        self.ssq = sb("ssq", [128, 2], F32)
        self.FT = min(512, TM)
        if NCH * TM >= NFC * self.FT:
            self.aT = self.oT[:, :, :].rearrange("p c t -> p (c t)")[:, 0:NFC * self.FT].rearrange(
                "p (f t) -> p f t", f=NFC)
        else:
            self.aT = sb("aT", [128, NFC, self.FT], BF16)
        self.ps = [self.ctx.enter_context(self.nc.psum_tensor(f"ps{i}", [128, 512], F32)) for i in range(8)]
        self.rings = {'a': [0, 1, 2, 3], 'acc': [4, 5, 6, 7], 'm': [6, 7]}

    def ident_b(self, n=128):
        return self.cB[0:n, 0:n]

    def triu_m8(self):
        return self.cB[:, 128:256]

    def ones_m8(self):
        return self.cB[:, 256:384]

    def ones_b(self):
        return self.cB[:, 384:512]

    def maskw(self, d, le):
        off = (1410 if le else 512) + 384 - 128 * d
        return self.cB[:, off:off + 512]

    def ident_f(self, n=128):
        return self.cF[0:n, 0:n]

    def g(self, vi, c):
        return self.gv[:, vi * 8 + c: vi * 8 + c + 1]

    def setup(self):
        self.dma('pool', self.cB[:], self.cb16, r=(), w=['cB'])
        self.dma('sp', self.cF[:], self.cf32, r=(), w=['cF'])
        self.dma('pool', self.sel[:], self.csel, r=(), w=['sel'])
        self.dma('sp', self.gv[:], self.gvec, r=(), w=['gv'])
        bsrc = bass.AP(self.b_f.tensor, 0, [[0, 128], [1, 16]])
        self.dma('sp', self.bfb[:], bsrc, r=(), w=['bfb'])
        self.convert_units()
        self.P.barrier()

    JS = {0: 0, 1: 1, 2: 1, 3: 2, 4: 3}

    def bA(self):
        return self.biasA[:, 0:4096].rearrange("p (h j q) -> p h j q", h=8, j=4)

    def setup_bias(self):
        bA = self.bA()
        for j in (0, 1, 3, 4):
            src = bass.AP(self.relext.tensor, 128 * j, [[1, 128], [LEXT, 8], [1, 128]])
            self.dma('sp', self.toe[:], src, r=(), w=['xst0', 'xst1'])
            for h in range(8):
                t0 = self.toe[:, h, :]
                rev = bass.AP(t0.tensor, t0.offset + 127, [list(t0.ap[0]), [-1, 128]])
                self.ts('dve', bA[:, h, self.JS[j], :], rev, 8.0, None, ALU.mult, None, r=['xst0', 'xst1'], w=['biasA'])
        for h in range(8):
            self.memset('dve', bA[0:64, h, 0, 64:128], MASKV, w=['biasA'])
            self.memset('dve', bA[64:128, h, 3, 0:64], MASKV, w=['biasA'])

    def pipe(self, items, stages, lags):
        n = len(items)
        for step in range(n + max(lags)):
            for fn, lag in zip(stages, lags):
                k = step - lag
                if 0 <= k < n:
                    fn(items[k])

    def wslot(self):
        i = self.wrp % len(self.wsl)
        self.wrp += 1
        return i

    def unit_specs(self):
        U = {}

        def cols(w2d, col0, ncols, off, width):
            return (w2d[:, col0:col0 + ncols].rearrange("(c p) n -> p c n", p=128), 8, width, off, ncols)
        order = []

        def put(k, v):
            U[k] = v
            order.append(k)
        for hp in range(8):
            isA = hp < 4
            hq = hp if isA else hp - 4
            base = 0 if isA else 1536
            put(('inab', hp), (3072, [cols(self.w_in_ab, base + hq * 128, 128, 0, 384),
                                       cols(self.w_in_ab, base + 512 + hq * 128, 128, 128, 384),
                                       cols(self.w_in_ab, base + 1024 + hq * 128, 128, 256, 384)]))
        for half in range(2):
            put(('res', 'oab', half), (4096, [cols(self.w_out_ab, half * 512, 512, 0, 512)]))
        for l in range(2):
            if l == 1:
                put(('incf',), (128, [cols(self.w_in_c, 3072, 16, 0, 16)]))
                for hp in range(8):
                    put(('inc', hp), (3072, [cols(self.w_in_c, hp * 128, 128, 0, 384),
                                              cols(self.w_in_c, 1024 + hp * 128, 128, 128, 384),
                                              cols(self.w_in_c, 2048 + hp * 128, 128, 256, 384)]))
                for half in range(2):
                    put(('res', 'oc', half), (4096, [cols(self.w_out_c, half * 512, 512, 0, 512)]))
            for wh, w3 in enumerate((self.w_xk, self.w_xv)):
                for half in range(2):
                    put(('xkv', wh, l, half), (4096, [cols(w3[l], half * 512, 512, 0, 512)]))
            for h in range(4):
                put(('xq', l, h), (2048, [cols(self.w_xq[l], h * 256, 256, 0, 256)]))
            for half in range(2):
                put(('res', f'xo{l}', half), (4096, [cols(self.w_xo[l], half * 512, 512, 0, 512)]))
            for blk in range(11):
                put(('gu', l, blk), (4096, [cols(self.w_gate[l], blk * 256, 256, 0, 512),
                                             cols(self.w_up[l], blk * 256, 256, 256, 512)]))
            for fcb in range(6):
                nf = min(4, NFC - fcb * 4)
                src = self.w_down[l][fcb * 512:fcb * 512 + nf * 128, :].rearrange("(c p) n -> p c n", p=128)
                put(('dn', l, fcb), (nf * 1024, [(src, nf, 1024, 0, 1024)]))
        return U, order

    def convert_units(self):
        self.U, order = self.unit_specs()
        self.uscr = {}
        for i, k in enumerate(order):
            nelem, parts = self.U[k]
            scr = self.nc.dram_tensor(f"wu{i}", [128, nelem], BF16).ap()
            self.uscr[k] = scr
            for pi, (src, C, width, off, ncols) in enumerate(parts):
                dst = scr[:, 0:C * width].rearrange("p (c n) -> p c n", c=C)[:, :, off:off + ncols]
                self.dma('pool', dst, src, r=(), w=[('wu', k, pi)], nobar=True)

    def load_unit(self, key, slot, C, width):
        nelem, parts = self.U[key]
        self.dma('sp', self.wsl[slot][:, 0:nelem], self.uscr[key], r=[('wu', key, pi) for pi in range(len(parts))],
                 w=[f'ws{slot}'])
        return self.wsl[slot][:, 0:C * width].rearrange("p (c n) -> p c n", c=C)

    def load_x(self, src, Tq):
        TK = min(128, Tq)
        for i in range(Tq // TK):
            for half in range(2):
                xh = self.xst[:, half * 512:(half + 1) * 512]
                self.dma('sp', xh[0:TK, :], src[i * TK:(i + 1) * TK, half * 512:(half + 1) * 512], r=(),
                         w=[f'xst{half}'])
                b = self.bank('m')
                for cc in range(4):
                    self.tr(self.ps[b][:, cc * TK:(cc + 1) * TK], xh[0:TK, cc * 128:(cc + 1) * 128],
                            self.ident_f(TK), r=[f'xst{half}', 'cF'], w=[f'ps{b}'])
                dst = self.xT[:, half * 4:half * 4 + 4, i * TK:(i + 1) * TK]
                srcp = self.ps[b][:, 0:4 * TK].rearrange("p (c t) -> p c t", c=4)
                self.copy('act', dst, srcp, r=[f'ps{b}'],
                          w=[('xT', c, (i * TK) // 512) for c in range(half * 4, half * 4 + 4)])

    def norm(self, vi, Tq):
        TT = min(512, Tq)
        ns = self.cfg.get('nstop', 9)
        for t in range(Tq // TT):
            sl = slice(t * TT, (t + 1) * TT)
            b = self.bank('m')
            for c in range(NCH):
                k = c % 2
                self.act(self.sqr[k][:, 0:TT], self.xT[:, c, sl], AF.Square, r=[('xT', c, t)], w=[('sq', k)],
                         scale=1.0 / 32.0)
                if ns < 1:
                    continue
                self.mm(self.ps[b][:, 0:TT], self.ones_b(), self.sqr[k][:, 0:TT], c == 0, c == NCH - 1,
                        r=[('sq', k), 'cB'], w=[f'ps{b}'])
            if ns < 2:
                continue
            self.act(self.rstd[:, 0:TT], self.ps[b][:, 0:TT], AF.Ln, r=[f'ps{b}'], w=[('rden', 0), ('rden', 1)], bias=EPS)
            if ns < 3:
                continue
            self.act(self.rstd[:, 0:TT], self.rstd[:, 0:TT], AF.Exp, r=[('rden', 0), ('rden', 1)], w=[('rden', 0), ('rden', 1)], scale=-0.5)
            if ns < 4:
                continue
            for c in range(NCH):
                self.stt('dve', self.hT[:, c, sl], self.xT[:, c, sl], self.g(vi, c), self.rstd[:, 0:TT],
                         ALU.mult, ALU.mult, r=[('xT', c, t), ('rden', 0), ('rden', 1), 'gv'], w=[('hT', c, t)])

    def proj_fm(self, wv, woff, dst_fn, Tq, wkey, dkey_fn):
        TT = min(512, Tq)
        for t in range(Tq // TT):
            sl = slice(t * TT, (t + 1) * TT)
            b = self.bank('a')
            for c in range(NCH):
                self.mm(self.ps[b][:, 0:TT], wv[:, c, woff:woff + 128], self.hT[:, c, sl], c == 0, c == NCH - 1,
                        r=[wkey, ('hT', c, t)], w=[f'ps{b}'])
            self.copy('act', dst_fn(sl), self.ps[b][:, 0:TT], r=[f'ps{b}'], w=[dkey_fn(t)])

    def proj_res(self, uname, srcT, skey_fn, Tq):
        TT = min(512, Tq)
        for half in range(2):
            s = self.wslot()
            wv = self.load_unit(('res', uname, half), s, 8, 512)
            for nn in range(4):
                n = half * 4 + nn
                for t in range(Tq // TT):
                    sl = slice(t * TT, (t + 1) * TT)
                    b = self.bank('a')
                    for c in range(NCH):
                        self.mm(self.ps[b][:, 0:TT], wv[:, c, nn * 128:(nn + 1) * 128], srcT[:, c, sl],
                                c == 0, c == NCH - 1, r=[f'ws{s}', skey_fn(c, t)], w=[f'ps{b}'])
                    self.tt('dve', self.xT[:, n, sl], self.ps[b][:, 0:TT], self.xT[:, n, sl], ALU.add,
                            r=[f'ps{b}', ('xT', n, t)], w=[('xT', n, t)])

    def kv_tm(self, wv, koff, Tq, ktile0, kout, vout, hpcol, keep_from=0):
        TK = min(128, Tq)
        n = Tq // TK
        sts = {}

        def front(i):
            sl = slice(i * TK, (i + 1) * TK)
            t = (i * TK) // 512
            b = self.bank('a')
            for c in range(NCH):
                self.mm(self.ps[b][0:TK, 0:384], self.hT[:, c, sl], wv[:, c, 0:384], c == 0, c == NCH - 1,
                        r=[self.cur_wkey, ('hT', c, t)], w=[f'ps{b}'])
            si = self.kvrp % 4
            self.kvrp += 1
            sts[i] = si
            st, skey = self.kvring()[si]
            self.copy('dve', st[0:TK, :], self.ps[b][0:TK, 0:384], r=[f'ps{b}'], w=[skey])
            kt = ktile0 + i
            self.copy('pool', self.vS[0:TK, kt, 0:64], st[0:TK, 256:320], r=[skey], w=[('vS', kt)])
            self.copy('pool', self.vS[0:TK, kt, 128:192], st[0:TK, 320:384], r=[skey], w=[('vS', kt)])
            if i * TK >= keep_from:
                r0 = i * TK - keep_from
                self.dma('sp', kout[r0:r0 + TK, hpcol:hpcol + 128], st[0:TK, 128:256], r=[skey], w=())
                self.dma('sp', vout[r0:r0 + TK, hpcol:hpcol + 128], st[0:TK, 256:384], r=[skey], w=())

        def back(i):
            si = sts[i]
            st, skey = self.kvring()[si]
            kt = ktile0 + i
            t = (i * TK) // 512
            b2 = self.bank('m')
            self.tr(self.ps[b2][:, 0:TK], st[0:TK, 0:128], self.ident_f(TK), r=[skey, 'cF'], w=[f'ps{b2}'])
            self.tr(self.ps[b2][:, TK:2 * TK], st[0:TK, 128:256], self.ident_f(TK), r=[skey, 'cF'],
                    w=[f'ps{b2}'])
            c0 = self.kcol(kt)
            self.copy('act', self.qT[:, i * TK:(i + 1) * TK], self.ps[b2][:, 0:TK], r=[f'ps{b2}'], w=[('qT', t)])
            self.copy('act', self.kT[:, c0:c0 + TK], self.ps[b2][:, TK:2 * TK], r=[f'ps{b2}'], w=[('kT', kt)])

        self.pipe(list(range(n)), [front, back], [0, 1])

    def kvring(self):
        return [(self.kvst[0], 'kvst0'), (self.kvst[1], 'kvst1'),
                (self.tmpf[0][:, 0:384], 'tmpf0'), (self.tmpf[1][:, 0:384], 'tmpf1')]

    def kcol(self, kt):
        return kt * 128

    def load_cache_kv(self, ck, cv, npast, hpcol):
        for i in range(npast // 128):
            si = self.kvrp % 2
            self.kvrp += 1
            st = self.kvst[si]
            self.dma('sp', st[:, 0:128], ck[i * 128:(i + 1) * 128, hpcol:hpcol + 128], r=(), w=[f'kvst{si}'])
            b2 = self.bank('m')
            self.tr(self.ps[b2][:, 0:128], st[:, 0:128], self.ident_f(), r=[f'kvst{si}', 'cF'], w=[f'ps{b2}'])
            self.copy('act', self.kT[:, i * 128:(i + 1) * 128], self.ps[b2][:, 0:128], r=[f'ps{b2}'], w=[('kT', i)])
            self.dma('pool', self.vS[:, i, 0:64], cv[i * 128:(i + 1) * 128, hpcol:hpcol + 64], r=(), w=[('vS', i)])
            self.dma('pool', self.vS[:, i, 128:192], cv[i * 128:(i + 1) * 128, hpcol + 64:hpcol + 128], r=(),
                     w=[('vS', i)])

    def finish_head(self, bacc, hh, hp, sl, t, n):
        if hh == 0:
            num, den, dst = self.ps[bacc][0:64, 0:n], self.ps[bacc][64:128, 0:n], self.oT[0:64, hp, sl]
            rd = self.rden[0:64, 0:n]
        else:
            num, den, dst = self.ps[bacc][64:128, 0:n], self.ps[bacc][0:64, 0:n], self.oT[64:128, hp, sl]
            rd = self.rden[64:128, 0:n]
        if hh == 0:
            self.P.add('dve', lambda e: e.reciprocal(out=rd, in_=den), r=[f'ps{bacc}'], w=[('rden', hh)])
        else:
            self.act(rd, den, AF.Ln, r=[f'ps{bacc}'], w=[('rden', hh)])
            self.act(rd, rd, AF.Exp, r=[('rden', hh)], w=[('rden', hh)], scale=-1.0)
        self.tt('dve', dst, num, rd, ALU.mult, r=[f'ps{bacc}', ('rden', hh)], w=[('oT', hp, t), 'aT_all'])

    def vlhs(self, kt, hh, rows):
        return self.vS[0:rows, kt, hh * 64:hh * 64 + 128]

    def set_ones(self):
        self.memset('dve', self.vS[:, :, 64:128], 1.0, w=['vS_ones'])

    def attn_A(self, hp, Tq, npast):
        bA = self.bA()
        if npast == 0:
            tiles = [(pi * 128, 128, [(j, pi + j - 4, 128) for j in range(5) if pi + j - 4 >= 0])
                     for pi in range(Tq // 128)]
        else:
            tiles = [(0, Tq, [(j, j, 128) for j in range(4)] + [(4, 4, Tq)])]
        items = []
        for (q0, nq, blocks) in tiles:
            for hh in range(2):
                g = dict(q0=q0, nq=nq, hh=hh, nb=len(blocks))
                for bi, (j, kt, nk) in enumerate(blocks):
                    items.append(dict(g=g, bi=bi, j=j, kt=kt, nk=nk))

        def s0(it):
            g = it['g']
            hh, q0, nq, nk, kt = g['hh'], g['q0'], g['nq'], it['nk'], it['kt']
            h = hp * 2 + hh
            rows = slice(hh * 64, hh * 64 + 64)
            t = q0 // 512
            b = self.bank('a')
            c0 = self.kcol(kt)
            self.mm(self.ps[b][0:nk, 0:nq], self.kT[rows, c0:c0 + nk], self.qT[rows, q0:q0 + nq], True, False,
                    r=[('kT', kt), ('qT', t)], w=[f'ps{b}'])
            self.mm(self.ps[b][0:nk, 0:nq], self.ident_b(nk), bA[0:nk, h, self.JS[it['j']], 0:nq], False, True,
                    r=['cB', 'biasA'], w=[f'ps{b}'])
            wi = self.wtrp % 4
            self.wtrp += 1
            it['wi'] = wi
            self.act(self.wtb[wi][0:nk, 0:nq], self.ps[b][0:nk, 0:nq], AF.Exp, r=[f'ps{b}'], w=[f'wtb{wi}'],
                     scale=0.125)

        def s1(it):
            g = it['g']
            hh, q0, nq, nk, kt, wi = g['hh'], g['q0'], g['nq'], it['nk'], it['kt'], it['wi']
            if it['bi'] == 0:
                g['bacc'] = self.bank('acc')
            bacc = g['bacc']
            self.mm(self.ps[bacc][:, 0:nq], self.vlhs(kt, hh, nk), self.wtb[wi][0:nk, 0:nq],
                    it['bi'] == 0, it['bi'] == g['nb'] - 1, r=[('vS', kt), 'vS_ones', f'wtb{wi}'], w=[f'ps{bacc}'])
            if it['bi'] == g['nb'] - 1:
                self.finish_head(bacc, hh, hp, slice(q0, q0 + nq), q0 // 512, nq)

        self.pipe(items, [s0, s1], [0, 2])

    def attn_B(self, hp, Tq, npast):
        TT = min(512, Tq)
        npt = npast // 128
        streams = []
        for t in reversed(range(Tq // TT)):
            q0 = t * TT
            for hh in range(2):
                blocks = []
                if npast == 0:
                    for kt in range((q0 + TT) // 128 - 1, -1, -1):
                        d = kt - q0 // 128
                        blocks.append((kt, 128, d if d >= 0 else None))
                else:
                    blocks.append((npt, Tq, 0))
                    for kt in range(npt - 1, -1, -1):
                        blocks.append((kt, 128, None))
                streams.append(dict(t=t, hh=hh, q0=q0, blocks=blocks, nb=len(blocks)))
        NSL = 4
        pending = list(streams)
        active = [None] * NSL
        items = []
        while True:
            for sl in range(NSL):
                if active[sl] is None and pending:
                    st_ = pending.pop(0)
                    st_['slot'] = sl
                    st_['pos'] = 0
                    active[sl] = st_
            if all(a is None for a in active):
                break
            for sl in range(NSL):
                st_ = active[sl]
                if st_ is None:
                    continue
                kt, nk, d = st_['blocks'][st_['pos']]
                items.append(dict(s=st_, bi=st_['pos'], kt=kt, nk=nk, d=d))
                st_['pos'] += 1
                if st_['pos'] == st_['nb']:
                    active[sl] = None

        def s0(it):
            st_ = it['s']
            hh, q0, t, kt, nk, d = st_['hh'], st_['q0'], st_['t'], it['kt'], it['nk'], it['d']
            rows = slice(hh * 64, hh * 64 + 64)
            c0 = self.kcol(kt)
            b = self.bank('a')
            it['b'] = b
            x0 = 128 * d if (d is not None and npast == 0) else 0
            it['x0'] = x0
            pz = self.ps[b][0:nk, x0:TT]
            self.mm(pz, self.kT[rows, c0:c0 + nk], self.qT[rows, q0 + x0:q0 + TT], True, True,
                    r=[('kT', kt), ('qT', t)], w=[f'ps{b}'])
            ti = self.tfrp % 2
            self.tfrp += 1
            tf = self.tmpf[ti]
            self.act(tf[0:nk, x0:TT], pz, AF.Exp, r=[f'ps{b}'], w=[f'tmpf{ti}'], scale=0.125)
            si = self.sprp % 4
            self.sprp += 1
            it['si'] = si
            sp_ = self.spb[si]
            self.act(sp_[0:nk, x0:TT], tf[0:nk, x0:TT], AF.Ln, r=[f'tmpf{ti}'], w=[f'spb{si}'], bias=1.0)
            if d is not None:
                self.tt('dve', sp_[0:nk, x0:TT], sp_[0:nk, x0:TT], self.maskw(d, False)[0:nk, x0:TT], ALU.mult,
                        r=[f'spb{si}', 'cB'], w=[f'spb{si}'])

        def s1(it):
            st_ = it['s']
            sl, bi, nb, nk, d, b, si, x0 = st_['slot'], it['bi'], st_['nb'], it['nk'], it['d'], it['b'], it['si'], it['x0']
            pz = self.ps[b][0:nk, x0:TT]
            sp_ = self.spb[si]
            ra = self.racc[sl]
            self.mm(pz, self.triu_m8()[0:nk, 0:nk], sp_[0:nk, x0:TT], False, bi == 0,
                    r=['cB', f'spb{si}'], w=[f'ps{b}'])
            if bi > 0:
                self.mm(pz, self.ones_m8()[0:st_['rrows'], 0:nk], ra[0:st_['rrows'], x0:TT], False, True,
                        r=['cB', f'racc{sl}'], w=[f'ps{b}'])
            if bi == 0:
                if nk < 128 or x0 > 0:
                    self.memset('pool', ra[:, 0:TT], 0.0, w=[f'racc{sl}'])
                self.copy('pool', ra[0:nk, x0:TT], sp_[0:nk, x0:TT], r=[f'spb{si}'], w=[f'racc{sl}'])
                st_['rrows'] = 128 if nb > 1 else nk
            elif bi < nb - 1:
                self.tt('pool', ra[0:nk, x0:TT], ra[0:nk, x0:TT], sp_[0:nk, x0:TT], ALU.add,
                        r=[f'racc{sl}', f'spb{si}'], w=[f'racc{sl}'])
            wi = self.wtrp % 4
            self.wtrp += 1
            it['wi'] = wi
            wt = self.wtb[wi]
            self.act(wt[0:nk, x0:TT], pz, AF.Exp, r=[f'ps{b}'], w=[f'wtb{wi}'], scale=0.125)
            if d is not None:
                self.tt('dve', wt[0:nk, x0:TT], wt[0:nk, x0:TT], self.maskw(d, False)[0:nk, x0:TT], ALU.mult,
                        r=[f'wtb{wi}', 'cB'], w=[f'wtb{wi}'])

        def s2(it):
            st_ = it['s']
            sl, bi, nb, nk, kt, hh, q0, t = st_['slot'], it['bi'], st_['nb'], it['nk'], it['kt'], st_['hh'], st_['q0'], st_['t']
            x0, d = it['x0'], it['d']
            bacc = 4 + sl
            wt = self.wtb[it['wi']]
            rk = [('vS', kt), 'vS_ones', f"wtb{it['wi']}"]
            if bi == 0:
                if x0 > 0:
                    self.memset('dve', wt[0:nk, 0:x0], 0.0, w=[f"wtb{it['wi']}"])
                self.mm(self.ps[bacc][:, 0:TT], self.vlhs(kt, hh, nk), wt[0:nk, 0:TT], True, bi == nb - 1,
                        r=rk, w=[f"ps{bacc}"])
            else:
                self.mm(self.ps[bacc][:, x0:TT], self.vlhs(kt, hh, nk), wt[0:nk, x0:TT], False, bi == nb - 1,
                        r=rk, w=[f"ps{bacc}"])
            if bi == nb - 1:
                rs = slice(hh * 64, hh * 64 + 64)
                self.copy('dve', self.oT[rs, hp, q0:q0 + TT], self.ps[bacc][rs, 0:TT],
                          r=[f"ps{bacc}"], w=[('oT', hp, t), 'aT_all'])

        self.pipe(items, [s0, s1, s2], [0, 2, 4])

    def attn_C(self, hp, Tq, npast):
        TT = min(512, Tq)
        npt = npast // 128
        selv = self.sel[:].rearrange("p (h n) -> p h n", h=16)
        items = []
        for t in range(Tq // TT):
            q0 = t * TT
            for hh in range(2):
                if npast == 0:
                    blocks = [(kt, 128, (kt - q0 // 128) if kt >= q0 // 128 else None)
                              for kt in range((q0 + TT) // 128)]
                else:
                    blocks = [(kt, 128, None) for kt in range(npt)] + [(npt, Tq, 0)]
                g = dict(q0=q0, t=t, hh=hh, nb=len(blocks))
                for bi, (kt, nk, d) in enumerate(blocks):
                    items.append(dict(g=g, bi=bi, kt=kt, nk=nk, d=d))

        def s0(it):
            g = it['g']
            hh, q0, t, kt, nk, d = g['hh'], g['q0'], g['t'], it['kt'], it['nk'], it['d']
            h = hp * 2 + hh
            rows = slice(hh * 64, hh * 64 + 64)
            c0 = self.kcol(kt)
            b = self.bank('a')
            x0 = 128 * d if d is not None else 0
            it['x0'] = x0
            pz = self.ps[b][0:nk, x0:TT]
            self.mm(pz, self.kT[rows, c0:c0 + nk], self.qT[rows, q0 + x0:q0 + TT], True, False,
                    r=[('kT', kt), ('qT', t)], w=[f'ps{b}'])
            self.mm(pz, selv[:, h, 0:nk], self.cumT8[:, q0 + x0:q0 + TT], False, True,
                    r=['sel', ('cumT8', t)], w=[f'ps{b}'])
            wi = self.wtrp % 4
            self.wtrp += 1
            it['wi'] = wi
            wt = self.wtb[wi]
            self.act(wt[0:nk, x0:TT], pz, AF.Exp, r=[f'ps{b}', ('negcum', kt)], w=[f'wtb{wi}'],
                     bias=self.negcum[0:nk, kt, h:h + 1], scale=0.125)
            if d is not None:
                self.tt('dve', wt[0:nk, x0:TT], wt[0:nk, x0:TT], self.maskw(d, True)[0:nk, x0:TT], ALU.mult,
                        r=[f'wtb{wi}', 'cB'], w=[f'wtb{wi}'])

        def s1(it):
            g = it['g']
            hh, q0, t, kt, nk, x0, wi = g['hh'], g['q0'], g['t'], it['kt'], it['nk'], it['x0'], it['wi']
            if it['bi'] == 0:
                g['bacc'] = self.bank('acc')
            bacc = g['bacc']
            self.mm(self.ps[bacc][:, x0:TT], self.vlhs(kt, hh, nk), self.wtb[wi][0:nk, x0:TT],
                    it['bi'] == 0, it['bi'] == g['nb'] - 1, r=[('vS', kt), 'vS_ones', f'wtb{wi}'], w=[f'ps{bacc}'])
            if it['bi'] == g['nb'] - 1:
                self.finish_head(bacc, hh, hp, slice(q0, q0 + TT), t, TT)

        self.pipe(items, [s0, s1], [0, 2])

    def gate_cum(self, wf_v, Tq, npast, lf_out):
        TK = min(128, Tq)
        npt = npast // 128
        ntile_new = Tq // TK
        for i in range(npt):
            self.dma('sp', self.lfs[:, i, :], self.cc_lf[i * 128:(i + 1) * 128, :], r=(), w=[('lfs', i)])
        for i in range(ntile_new):
            kt = npt + i
            sl = slice(i * TK, (i + 1) * TK)
            t = (i * TK) // 512
            b = self.bank('m')
            for c in range(NCH):
                self.mm(self.ps[b][0:TK, 0:16], self.hT[:, c, sl], wf_v[:, c, 0:16], c == 0, c == NCH - 1,
                        r=[self.cur_wkey, ('hT', c, t)], w=[f'ps{b}'])
            z = self.sm[0:TK, 0:16]
            self.tt('dve', z, self.ps[b][0:TK, 0:16], self.bfb[0:TK, :], ALU.add, r=[f'ps{b}', 'bfb'], w=['sm_z'])
            e_ = self.sm[0:TK, 16:32]
            self.act(e_, z, AF.Exp, r=['sm_z'], w=['sm_e'], scale=-1.0)
            l_ = self.sm[0:TK, 32:48]
            self.act(l_, e_, AF.Ln, r=['sm_e'], w=['sm_l'], bias=1.0)
            self.ts('dve', self.lfs[0:TK, kt, :], l_, -1.0, None, ALU.mult, None, r=['sm_l'], w=[('lfs', kt)])
            self.dma('sp', lf_out[i * TK:(i + 1) * TK, :], self.lfs[0:TK, kt, :], r=[('lfs', kt)], w=())
        nkt = npt + ntile_new
        self.memset('dve', self.cumT8[32:64, 0:Tq], 0.0, w=[('cumT8', t_) for t_ in range(max(1, Tq // 512))])
        self.memset('dve', self.cumT8[64:128, 0:Tq], 0.0, w=[('cumT8', t_) for t_ in range(max(1, Tq // 512))])
        for kt in range(nkt):
            nk = 128 if kt < npt else TK
            b = self.bank('m')
            pc = self.ps[b][0:nk, 0:16]
            for k2 in range(kt):
                n2 = 128 if k2 < npt else TK
                self.mm(pc, self.cF[0:n2, 256:256 + nk], self.lfs[0:n2, k2, :], k2 == 0, False,
                        r=['cF', ('lfs', k2)], w=[f'ps{b}'])
            self.mm(pc, self.cF[0:nk, 128:128 + nk], self.lfs[0:nk, kt, :], kt == 0, True,
                    r=['cF', ('lfs', kt)], w=[f'ps{b}'])
            self.ts('dve', self.negcum[0:nk, kt, :], pc, -1.0, None, ALU.mult, None, r=[f'ps{b}'], w=[('negcum', kt)])
            if kt >= npt:
                i = kt - npt
                t = (i * TK) // 512
                c8 = self.sm[0:TK, 48:64]
                self.ts('dve', c8, pc, 8.0, None, ALU.mult, None, r=[f'ps{b}'], w=['sm_c8'])
                spl = self.sm[0:TK, 64:112].rearrange("p (h j) -> p h j", j=3)
                tb = self.sm[0:TK, 128:144]
                self.copy('dve', self.b16tmp[0:TK, :], c8, r=['sm_c8'], w=['b16tmp'])
                self.copy('dve', spl[:, :, 0], self.b16tmp[0:TK, :], r=['b16tmp'], w=['sm_spl'])
                self.tt('dve', tb, c8, spl[:, :, 0], ALU.subtract, r=['sm_c8', 'sm_spl'], w=['sm_tb'])
                self.copy('dve', self.b16tmp[0:TK, :], tb, r=['sm_tb'], w=['b16tmp'])
                self.copy('dve', spl[:, :, 1], self.b16tmp[0:TK, :], r=['b16tmp'], w=['sm_spl'])
                self.tt('dve', tb, tb, spl[:, :, 1], ALU.subtract, r=['sm_tb', 'sm_spl'], w=['sm_tb'])
                self.copy('dve', self.b16tmp[0:TK, :], tb, r=['sm_tb'], w=['b16tmp'])
                self.copy('dve', spl[:, :, 2], self.b16tmp[0:TK, :], r=['b16tmp'], w=['sm_spl'])
                b2 = self.bank('m')
                self.tr(self.ps[b2][0:48, 0:TK], self.sm[0:TK, 64:112], self.ident_f(TK), r=['sm_spl', 'cF'],
                        w=[f'ps{b2}'])
                self.copy('dve', self.cumT8[0:48, i * TK:(i + 1) * TK], self.ps[b2][0:48, 0:TK], r=[f'ps{b2}'],
                          w=[('cumT8', t)])

    def mem_kv_prompt(self, layer, s):
        mhT = self.mhT
        for mt in range(2):
            self.dma('sp', self.xst[:, :], self.mem[s, mt * 128:(mt + 1) * 128, :], r=(), w=['xst0', 'xst1'])
            for hf in range(2):
                self.act(self.tmpf[hf][:, 0:512], self.xst[:, hf * 512:(hf + 1) * 512], AF.Square, r=['xst0', 'xst1'],
                         w=[f'tmpf{hf}'])
                self.P.add('dve', lambda e, hf=hf: e.reduce_sum(out=self.ssq[:, hf:hf + 1], in_=self.tmpf[hf][:, 0:512],
                                                                 axis=mybir.AxisListType.X),
                           r=[f'tmpf{hf}'], w=['ssq'])
            self.tt('dve', self.ssq[:, 0:1], self.ssq[:, 0:1], self.ssq[:, 1:2], ALU.add, r=['ssq'], w=['ssq'])
            self.act(self.ssq[:, 0:1], self.ssq[:, 0:1], AF.Ln, r=['ssq'], w=['ssq'], bias=EPS, scale=1.0 / D)
            self.act(self.ssq[:, 0:1], self.ssq[:, 0:1], AF.Exp, r=['ssq'], w=['ssq'], scale=-0.5)
            self.ts('dve', self.xst[:, :], self.xst[:, :], self.ssq[:, 0:1], None, ALU.mult, None,
                    r=['xst0', 'xst1', 'ssq'], w=['xst0', 'xst1'])
            for half in range(2):
                b = self.bank('m')
                for cc in range(4):
                    c = half * 4 + cc
                    self.tr(self.ps[b][:, cc * 128:(cc + 1) * 128], self.xst[:, c * 128:(c + 1) * 128],
                            self.ident_f(), r=['xst0', 'xst1', 'cF'], w=[f'ps{b}'])
                for cc in range(4):
                    c = half * 4 + cc
                    self.ts('dve', mhT[:, c, mt * 128:(mt + 1) * 128], self.ps[b][:, cc * 128:(cc + 1) * 128],
                            self.g(4 + layer, c), None, ALU.mult, None, r=[f'ps{b}', 'gv'], w=['mhT'])
        for which, (w3, outd) in enumerate(((self.w_xk, self.mem_k_p), (self.w_xv, self.mem_v_p))):
            for half in range(2):
                sl_ = self.wslot()
                wv = self.load_unit(('xkv', which, layer, half), sl_, 8, 512)
                for mt in range(2):
                    b = self.bank('a')
                    for c in range(NCH):
                        self.mm(self.ps[b][:, 0:512], mhT[:, c, mt * 128:(mt + 1) * 128], wv[:, c, :], c == 0,
                                c == NCH - 1, r=[f'ws{sl_}', 'mhT'], w=[f'ps{b}'])
                    st = self.tmpf[1]
                    self.copy('dve', st[:, 0:512], self.ps[b][:, 0:512], r=[f'ps{b}'], w=['tmpf1'])
                    self.dma('sp', outd[layer, s, mt * 128:(mt + 1) * 128, half * 512:(half + 1) * 512], st[:, 0:512],
                             r=['tmpf1'], w=())
                    if which == 0:
                        b2 = self.bank('m')
                        for cc in range(4):
                            self.tr(self.ps[b2][:, cc * 128:(cc + 1) * 128], st[:, cc * 128:(cc + 1) * 128],
                                    self.ident_f(), r=['tmpf1', 'cF'], w=[f'ps{b2}'])
                        dst = self.mkT[:, half * 4:half * 4 + 4, mt * 128:(mt + 1) * 128]
                        self.copy('act', dst, self.ps[b2][:, 0:512].rearrange("p (c t) -> p c t", c=4),
                                  r=[f'ps{b2}'], w=['mkT'])
                    else:
                        self.copy('act', self.mv[:, mt, half * 512:(half + 1) * 512], st[:, 0:512], r=['tmpf1'],
                                  w=['mv'])

    def mem_kv_sample(self, layer):
        for mt in range(2):
            self.dma('sp', self.xst[:, :], self.cm_k[layer, mt * 128:(mt + 1) * 128, :], r=(), w=['xst0', 'xst1'])
            for half in range(2):
                b = self.bank('m')
                for cc in range(4):
                    c = half * 4 + cc
                    self.tr(self.ps[b][:, cc * 128:(cc + 1) * 128], self.xst[:, c * 128:(c + 1) * 128],
                            self.ident_f(), r=['xst0', 'xst1', 'cF'], w=[f'ps{b}'])
                dst = self.mkT[:, half * 4:half * 4 + 4, mt * 128:(mt + 1) * 128]
                self.copy('act', dst, self.ps[b][:, 0:512].rearrange("p (c t) -> p c t", c=4), r=[f'ps{b}'],
                          w=['mkT'])
            self.dma('pool', self.mv[:, mt, :], self.cm_v[layer, mt * 128:(mt + 1) * 128, :], r=(), w=['mv'])

    def xattn(self, layer, Tq):
        TT = min(512, Tq)
        self.norm(2 + layer, Tq)
        for h in range(4):
            s_ = self.wslot()
            wv = self.load_unit(('xq', layer, h), s_, 8, 256)
            self.cur_wkey = f'ws{s_}'
            for t in range(Tq // TT):
                sl = slice(t * TT, (t + 1) * TT)
                for cc in range(2):
                    b = self.bank('a')
                    for c in range(NCH):
                        self.mm(self.ps[b][:, 0:TT], wv[:, c, cc * 128:(cc + 1) * 128], self.hT[:, c, sl], c == 0,
                                c == NCH - 1, r=[f'ws{s_}', ('hT', c, t)], w=[f'ps{b}'])
                    self.copy('act', self.q2[:, cc, 0:TT], self.ps[b][:, 0:TT], r=[f'ps{b}'], w=[('q2', cc)])
                wts = []
                for mt in range(2):
                    b = self.bank('a')
                    for cc in range(2):
                        self.mm(self.ps[b][:, 0:TT], self.mkT[:, 2 * h + cc, mt * 128:(mt + 1) * 128],
                                self.q2[:, cc, 0:TT], cc == 0, cc == 1, r=['mkT', ('q2', cc)], w=[f'ps{b}'])
                    wi = self.wtrp % 4
                    self.wtrp += 1
                    self.act(self.wtb[wi][:, 0:TT], self.ps[b][:, 0:TT], AF.Exp, r=[f'ps{b}'], w=[f'wtb{wi}'],
                             scale=1.0 / 16.0)
                    wts.append(wi)
                bd = self.bank('m')
                for mt in range(2):
                    self.mm(self.ps[bd][:, 0:TT], self.ones_b(), self.wtb[wts[mt]][:, 0:TT], mt == 0, mt == 1,
                            r=['cB', f'wtb{wts[mt]}'], w=[f'ps{bd}'])
                self.P.add('dve', lambda e, bd=bd: e.reciprocal(out=self.rden[:, 0:TT], in_=self.ps[bd][:, 0:TT]),
                           r=[f'ps{bd}'], w=[('rden', 0), ('rden', 1)])
                for cc in range(2):
                    bo = self.bank('acc')
                    for mt in range(2):
                        self.mm(self.ps[bo][:, 0:TT], self.mv[:, mt, (2 * h + cc) * 128:(2 * h + cc + 1) * 128],
                                self.wtb[wts[mt]][:, 0:TT], mt == 0, mt == 1, r=['mv', f'wtb{wts[mt]}'],
                                w=[f'ps{bo}'])
                    self.tt('dve', self.oT[:, 2 * h + cc, sl], self.ps[bo][:, 0:TT], self.rden[:, 0:TT], ALU.mult,
                            r=[f'ps{bo}', ('rden', 0), ('rden', 1)], w=[('oT', 2 * h + cc, t), 'aT_all'])
        self.proj_res(f'xo{layer}', self.oT, lambda c, t: ('oT', c, t), Tq)

    def ffn(self, layer, Tq):
        FT = min(self.FT, Tq)
        self.norm(6 + layer, Tq)
        TTn = min(512, Tq)
        blocks = [(c0_, 256) for c0_ in range(0, DFF, 256)]
        for ft in range(Tq // FT):
            sl = slice(ft * FT, (ft + 1) * FT)
            tkey = (ft * FT) // TTn
            for blk, (c0, ncol) in enumerate(blocks):
                sg_ = self.wslot()
                su_ = sg_
                wg = self.load_unit(('gu', layer, blk), sg_, 8, 512)
                for j in range(ncol // 128):
                    fc = c0 // 128 + j
                    bg = self.bank('a')
                    bu = self.bank('a')
                    for c in range(NCH):
                        self.mm(self.ps[bg][:, 0:FT], wg[:, c, j * 128:(j + 1) * 128], self.hT[:, c, sl], c == 0,
                                c == NCH - 1, r=[f'ws{sg_}', ('hT', c, tkey)], w=[f'ps{bg}'])
                    for c in range(NCH):
                        self.mm(self.ps[bu][:, 0:FT], wg[:, c, 256 + j * 128:256 + (j + 1) * 128], self.hT[:, c, sl],
                                c == 0, c == NCH - 1, r=[f'ws{su_}', ('hT', c, tkey)], w=[f'ps{bu}'])
                    k = fc % 2
                    self.act(self.tmpf[k][:, 0:FT], self.ps[bg][:, 0:FT], AF.Silu, r=[f'ps{bg}'], w=[f'tmpf{k}'])
                    self.tt('dve', self.aT[:, fc, 0:FT], self.ps[bu][:, 0:FT], self.tmpf[k][:, 0:FT], ALU.mult,
                            r=[f'ps{bu}', f'tmpf{k}'], w=[('aT', fc)])
            for fcb in range(6):
                nf = min(4, NFC - fcb * 4)
                sd_ = self.wslot()
                wd = self.load_unit(('dn', layer, fcb), sd_, nf, 1024)
                for n in range(NCH):
                    for j in range(nf):
                        fc = fcb * 4 + j
                        self.mm(self.ps[n][:, 0:FT], wd[:, j, n * 128:(n + 1) * 128], self.aT[:, fc, 0:FT], fc == 0,
                                fc == NFC - 1, r=[f'ws{sd_}', ('aT', fc), 'aT_all'], w=[f'ps{n}'])
            for n in range(NCH):
                self.tt('dve', self.xT[:, n, sl], self.ps[n][:, 0:FT], self.xT[:, n, sl], ALU.add,
                        r=[f'ps{n}', ('xT', n, tkey)], w=[('xT', n, tkey)])

    def final_out(self, Tq, ydst):
        TT = min(512, Tq)
        TK = min(128, Tq)
        yT = self.tmpf
        for t in range(Tq // TT):
            sl = slice(t * TT, (t + 1) * TT)
            b = self.bank('m')
            for c in range(NCH):
                k = c % 2
                self.act(self.sqr[k][:, 0:TT], self.xT[:, c, sl], AF.Square, r=[('xT', c, t)], w=[('sq', k)],
                         scale=1.0 / 32.0)
                self.mm(self.ps[b][:, 0:TT], self.ones_b(), self.sqr[k][:, 0:TT], c == 0, c == NCH - 1,
                        r=[('sq', k), 'cB'], w=[f'ps{b}'])
            self.act(self.rstd[:, 0:TT], self.ps[b][:, 0:TT], AF.Ln, r=[f'ps{b}'], w=[('rden', 0), ('rden', 1)], bias=EPS)
            self.act(self.rstd[:, 0:TT], self.rstd[:, 0:TT], AF.Exp, r=[('rden', 0), ('rden', 1)], w=[('rden', 0), ('rden', 1)], scale=-0.5)
            for i in range(TT // TK):
                tok = slice(t * TT + i * TK, t * TT + (i + 1) * TK)
                for half in range(2):
                    b2 = self.bank('a')
                    for cc in range(4):
                        c = half * 4 + cc
                        k = c % 2
                        self.stt('dve', yT[k][:, 0:TK], self.xT[:, c, tok], self.g(8, c),
                                 self.rstd[:, i * TK:(i + 1) * TK], ALU.mult, ALU.mult,
                                 r=[('xT', c, t), ('rden', 0), ('rden', 1), 'gv'], w=[f'tmpf{k}'])
                        self.tr(self.ps[b2][0:TK, cc * 128:(cc + 1) * 128], yT[k][:, 0:TK], self.ident_f(),
                                r=[f'tmpf{k}', 'cF'], w=[f'ps{b2}'])
                    self.copy('act', self.xst[0:TK, half * 512:(half + 1) * 512], self.ps[b2][0:TK, 0:512],
                              r=[f'ps{b2}'], w=[f'xst{half}'])
                    self.dma('sp', ydst[t * TT + i * TK:t * TT + (i + 1) * TK, half * 512:(half + 1) * 512],
                             self.xst[0:TK, half * 512:(half + 1) * 512], r=[f'xst{half}'], w=())

    def run_seq(self, kind, s):
        if kind == 'p':
            Tq, xsrc = self.T, self.xp[s]
            pa = pb = pc = 0
        else:
            Tq, xsrc = self.TS, self.xs
            pa, pb, pc = self.PA, self.PB, self.PC
        TK = min(128, Tq)
        self.load_x(xsrc, Tq)
        self.ck(1)
        self.setup_bias()
        self.ck(2)
        self.set_ones()
        self.norm(0, Tq)
        self.ck(3)
        for hp in range(8):
            self.ck(4 + hp)
            isA = hp < 4
            hq = hp if isA else hp - 4
            base = 0 if isA else 1536
            s_ = self.wslot()
            wv = self.load_unit(('inab', hp), s_, 8, 384)
            self.cur_wkey = f'ws{s_}'
            npast = pa if isA else pb
            if kind == 'p':
                if isA:
                    kout, vout, keep_from = self.a_k_p[s], self.a_v_p[s], Tq - self.KEEP
                else:
                    kout, vout, keep_from = self.b_k_p[s], self.b_v_p[s], 0
            else:
                kout, vout, keep_from = (self.a_k_s, self.a_v_s, 0) if isA else (self.b_k_s, self.b_v_s, 0)
                ck, cv = (self.ca_k, self.ca_v) if isA else (self.cb_k, self.cb_v)
                self.load_cache_kv(ck, cv, npast, hq * 128)
            self.kv_tm(wv, 128, Tq, npast // 128, kout, vout, hq * 128, keep_from)
            if isA:
                self.attn_A(hp, Tq, npast)
            else:
                self.attn_B(hp, Tq, npast)
        self.ck(12)
        self.proj_res('oab', self.oT, lambda c, t: ('oT', c, t), Tq)
        self.ck(13)
        self.layer_tail(0, kind, s, Tq)
        self.ck(20)
        self.P.barrier()
        self.set_ones()
        self.norm(1, Tq)
        s_ = self.wslot()
        wf = self.load_unit(('incf',), s_, 8, 16)
        self.cur_wkey = f'ws{s_}'
        self.gate_cum(wf, Tq, pc, self.c_lf_p[s] if kind == 'p' else self.c_lf_s)
        self.ck(21)
        for hp in range(8):
            self.ck(22 + hp)
            s_ = self.wslot()
            wv = self.load_unit(('inc', hp), s_, 8, 384)
            self.cur_wkey = f'ws{s_}'
            if kind == 'p':
                kout, vout = self.c_k_p[s], self.c_v_p[s]
            else:
                kout, vout = self.c_k_s, self.c_v_s
                self.load_cache_kv(self.cc_k, self.cc_v, pc, hp * 128)
            if self.cfg.get('cstop', 9) < 1:
                raise StopBuild()
            self.kv_tm(wv, 128, Tq, pc // 128, kout, vout, hp * 128, 0)
            if self.cfg.get('cstop', 9) < 2:
                raise StopBuild()
            self.attn_C(hp, Tq, pc)
        self.proj_res('oc', self.oT, lambda c, t: ('oT', c, t), Tq)
        self.layer_tail(1, kind, s, Tq)
        self.final_out(Tq, self.y_p[s] if kind == 'p' else self.y_s)

    def layer_tail(self, layer, kind, s, Tq):
        self.P.barrier()
        if kind == 'p':
            self.mem_kv_prompt(layer, s)
        else:
            self.mem_kv_sample(layer)
        self.ck(14 + 10 * layer)
        self.xattn(layer, Tq)
        self.ck(15 + 10 * layer)
        self.P.barrier()
        self.ffn(layer, Tq)
        self.ck(16 + 10 * layer)

    def build(self):
        with self.ctx:
            self.declare()
            self.b16tmp = self.sb("b16tmp", [128, 16], BF16)
            try:
                self.setup()
                self.ck(0)
                for s in range(self.NS):
                    self.run_seq('p', s)
                self.ck(100)
                self.run_seq('s', 0)
            except StopBuild:
                pass
            self.P.finish()
            self.P.emit(self.nc, self.ctx)
        return self.nc


def host_consts():
    import ml_dtypes
    cf32 = np.zeros((128, 384), np.float32)
    cf32[:, 0:128] = np.eye(128, dtype=np.float32)
    s = np.arange(128)[:, None]
    t = np.arange(128)[None, :]
    cf32[:, 128:256] = (s <= t).astype(np.float32)
    cf32[:, 256:384] = 1.0
    cb = np.zeros((128, 1410 + 896), np.float32)
    cb[:, 0:128] = np.eye(128)
    cb[:, 128:256] = -8.0 * (s >= t)
    cb[:, 256:384] = -8.0
    cb[:, 384:512] = 1.0
    u = np.arange(897)[None, :]
    cb[:, 512:1409] = (s < (u - 384)).astype(np.float32)
    u2 = np.arange(896)[None, :]
    cb[:, 1410:] = (s <= (u2 - 384)).astype(np.float32)
    sel = np.zeros((128, 16, 128), np.float32)
    for h in range(16):
        sel[3 * h:3 * h + 3, h, :] = 1.0
    return cf32, cb, sel.reshape(128, 16 * 128)


def relext_layout(rel):
    m = np.arange(LEXT)
    idx = np.clip(639 - m, -128, 128) + 128
    return np.ascontiguousarray(rel[idx, :].T)


_CACHE = {}


def get_nc(cfg):
    key = tuple(sorted(cfg.items()))
    if key not in _CACHE:
        _CACHE[key] = Builder(cfg).build()
    return _CACHE[key]


def make_in_maps(inp, cfg, ncores):
    NS = cfg['NS']
    cf32, cb, sel = host_consts()
    gnames = ['g_mix', 'g_xattn', 'g_mem', 'g_ffn']
    gv = np.zeros((128, 72), np.float32)
    vi = 0
    for nm in gnames:
        for l in range(2):
            gv[:, vi * 8:(vi + 1) * 8] = np.asarray(inp[nm][l]).reshape(8, 128).T
            vi += 1
    gv[:, 64:72] = np.asarray(inp['g_final']).reshape(8, 128).T
    f = lambda a: np.ascontiguousarray(np.asarray(a, dtype=np.float32))
    common = dict(
        w_in_ab=f(inp['w_in_ab'][0]), w_out_ab=f(inp['w_out_ab'][0]), relext=relext_layout(f(inp['rel_bias_a'][0])),
        w_in_c=f(inp['w_in_c'][0]), b_f=f(inp['b_f_c'][0]), w_out_c=f(inp['w_out_c'][0]), gvec=gv,
        w_xq=f(inp['w_xq']), w_xk=f(inp['w_xk']), w_xv=f(inp['w_xv']), w_xo=f(inp['w_xo']),
        w_gate=f(inp['w_gate']), w_up=f(inp['w_up']), w_down=f(inp['w_down']),
        cf32=cf32, cb16=cb, csel=sel)
    maps = []
    for i in range(ncores):
        m = dict(common)
        m['xp'] = f(inp['x_prompt'][i * NS:(i + 1) * NS])
        m['xs'] = f(inp['x_sample'][i])
        m['mem'] = f(inp['mem_prompt'][i * NS:(i + 1) * NS])
        m['ca_k'] = f(inp['cache_a_k'][0, i]).reshape(cfg['PA'], 512)
        m['ca_v'] = f(inp['cache_a_v'][0, i]).reshape(cfg['PA'], 512)
        m['cb_k'] = f(inp['cache_b_k'][0, i]).reshape(cfg['PB'], 512)
        m['cb_v'] = f(inp['cache_b_v'][0, i]).reshape(cfg['PB'], 512)
        m['cc_k'] = f(inp['cache_c_k'][0, i]).reshape(cfg['PC'], 1024)
        m['cc_v'] = f(inp['cache_c_v'][0, i]).reshape(cfg['PC'], 1024)
        m['cc_lf'] = f(inp['cache_c_logf'][0, i])
        m['cm_k'] = f(inp['cache_mem_k'][:, i]).reshape(2, NMEM, D)
        m['cm_v'] = f(inp['cache_mem_v'][:, i]).reshape(2, NMEM, D)
        maps.append(m)
    return maps


def gather(res, cfg, ncores):
    NS, T, TS = cfg['NS'], cfg['T'], cfg['TS']
    KEEP = min(512, T)
    R = res.results
    cat = lambda nm: np.concatenate([R[i][nm] for i in range(ncores)], axis=0)
    stk = lambda nm: np.stack([R[i][nm] for i in range(ncores)], axis=0)
    B = NS * ncores
    y_p = cat('y_p')
    y_s = stk('y_s')
    out = [y_p, y_s]
    out.append(cat('a_k_p').reshape(1, B, KEEP, 8, 64))
    out.append(cat('a_v_p').reshape(1, B, KEEP, 8, 64))
    out.append(cat('b_k_p').reshape(1, B, T, 8, 64))
    out.append(cat('b_v_p').reshape(1, B, T, 8, 64))
    out.append(cat('c_k_p').reshape(1, B, T, 16, 64))
    out.append(cat('c_v_p').reshape(1, B, T, 16, 64))
    out.append(cat('c_lf_p').reshape(1, B, T, 16))
    out.append(np.concatenate([R[i]['mem_k_p'] for i in range(ncores)], axis=1).reshape(2, B, NMEM, 4, 256))
    out.append(np.concatenate([R[i]['mem_v_p'] for i in range(ncores)], axis=1).reshape(2, B, NMEM, 4, 256))
    out.append(stk('a_k_s').reshape(1, ncores, TS, 8, 64))
    out.append(stk('a_v_s').reshape(1, ncores, TS, 8, 64))
    out.append(stk('b_k_s').reshape(1, ncores, TS, 8, 64))
    out.append(stk('b_v_s').reshape(1, ncores, TS, 8, 64))
    out.append(stk('c_k_s').reshape(1, ncores, TS, 16, 64))
    out.append(stk('c_v_s').reshape(1, ncores, TS, 16, 64))
    out.append(stk('c_lf_s').reshape(1, ncores, TS, 16))
    return tuple(np.ascontiguousarray(o, dtype=np.float32) for o in out)


def kernel(**inputs):
    ncores = 8
    cfg = dict(NS=4, T=2048, TS=32, PA=512, PB=1024, PC=1024)
    nc = get_nc(cfg)
    maps = make_in_maps(inputs, cfg, ncores)
    res = run_bass_kernel_spmd(nc, maps, core_ids=list(range(ncores)))
    return gather(res, cfg, ncores)
```

```python
import numpy as np
from contextlib import ExitStack
import concourse.bass as bass
import concourse.mybir as mybir
from concourse.bass_utils import run_bass_kernel_spmd

F32 = mybir.dt.float32
BF16 = mybir.dt.bfloat16
AF = mybir.ActivationFunctionType
ALU = mybir.AluOpType

NSLOT = 8
D = 1024
NCH = 8
DFF = 2816
NFC = 22
NMEM = 256
EPS = 1e-6
LEXT = 768
MASKV = -80000.0


class Prog:
    ENG = ['pe', 'act', 'dve', 'pool', 'sp']

    def __init__(self):
        self.ops = {e: [] for e in self.ENG}
        self.lastw = {}
        self.readers = {}
        self.ndma = {'sp': 0, 'pool': 0, 'act': 0}
        self.bar = {}
        self.bard = set()

    def add(self, eng, fn, r=(), w=(), dma=False, nobar=False):
        deps = dict(self.bar)
        ddeps = set(self.bard)

        def dep(p):
            e, i = p
            if self.ops[e][i]['dma']:
                ddeps.add(p)
            elif deps.get(e, -1) < i:
                deps[e] = i
        for b in r:
            if b in self.lastw:
                dep(self.lastw[b])
        for b in w:
            if b in self.lastw:
                dep(self.lastw[b])
            for p in self.readers.get(b, ()):
                dep(p)
        idx = len(self.ops[eng])
        if eng == 'pe':
            if 'pe' in self.bar:
                deps['pe'] = self.bar['pe']
            else:
                deps.pop('pe', None)
        op = dict(fn=fn, deps=deps, ddeps=ddeps, dma=dma, sig=False, nobar=nobar)
        if dma:
            k = self.ndma[eng]
            self.ndma[eng] = k + 1
            op['slot'] = k % NSLOT
            op['val'] = 16 * (k // NSLOT + 1)
        self.ops[eng].append(op)
        for b in r:
            self.readers.setdefault(b, []).append((eng, idx))
        for b in w:
            self.lastw[b] = (eng, idx)
            self.readers[b] = []
        return (eng, idx)

    def barrier(self):
        self.bar = {e: len(self.ops[e]) - 1 for e in self.ENG if len(self.ops[e]) > 0
                    and not all(o['dma'] for o in self.ops[e])}
        for e in list(self.bar):
            i = self.bar[e]
            while i >= 0 and self.ops[e][i]['dma']:
                i -= 1
            if i < 0:
                del self.bar[e]
            else:
                self.bar[e] = i
        self.bard = set()
        for q in ('sp', 'pool', 'act'):
            cnt = 0
            for i in range(len(self.ops[q]) - 1, -1, -1):
                if self.ops[q][i]['dma'] and not self.ops[q][i]['nobar']:
                    self.bard.add((q, i))
                    cnt += 1
                    if cnt >= NSLOT:
                        break

    def finish(self):
        tails = []
        for q in ('sp', 'pool', 'act'):
            cnt = 0
            for i in range(len(self.ops[q]) - 1, -1, -1):
                if self.ops[q][i]['dma']:
                    tails.append((q, i))
                    cnt += 1
                    if cnt >= NSLOT:
                        break
        op = dict(fn=lambda eng: eng.nop(), deps={}, ddeps=set(tails), dma=False, sig=False)
        self.ops['sp'].append(op)

    def emit(self, nc, ctx):
        ops = self.ops
        for e in self.ENG:
            for op in ops[e]:
                for pe_, pi in op['deps'].items():
                    ops[pe_][pi]['sig'] = True
        for e in self.ENG:
            c = 0
            for op in ops[e]:
                if op['dma']:
                    continue
                if op['sig']:
                    c += 1
                    op['val'] = c
        sems = {e: ctx.enter_context(nc.semaphore(f"s_{e}")) for e in self.ENG}
        dsems = {q: [ctx.enter_context(nc.semaphore(f"d_{q}{i}")) for i in range(NSLOT)]
                 for q in ('sp', 'pool', 'act') if self.ndma[q] > 0}
        block = ctx.enter_context(nc.Block())

        def run(e, eng):
            seen = {}
            dseen = {}
            for op in ops[e]:
                for pe_, pi in op['deps'].items():
                    v = ops[pe_][pi]['val']
                    if seen.get(pe_, 0) < v:
                        eng.wait_ge(sems[pe_], v)
                        seen[pe_] = v
                for (q, qi) in sorted(op['ddeps']):
                    d = ops[q][qi]
                    key = (q, d['slot'])
                    if dseen.get(key, 0) < d['val']:
                        eng.wait_ge(dsems[q][d['slot']], d['val'])
                        dseen[key] = d['val']
                if op['dma']:
                    key = (e, op['slot'])
                    pv = op['val'] - 16
                    if pv > 0 and dseen.get(key, 0) < pv:
                        eng.wait_ge(dsems[e][op['slot']], pv)
                        dseen[key] = pv
                    op['fn'](eng).then_inc(dsems[e][op['slot']], 16)
                else:
                    ins = op['fn'](eng)
                    if op['sig']:
                        ins.then_inc(sems[e], 1)

        block.tensor(lambda eng: run('pe', eng))
        block.scalar(lambda eng: run('act', eng))
        block.vector(lambda eng: run('dve', eng))
        block.gpsimd(lambda eng: run('pool', eng))
        block.sync(lambda eng: run('sp', eng))


class StopBuild(Exception):
    pass


class Builder:
    def ck(self, n):
        if self.cfg.get('stop', 10 ** 9) <= n:
            raise StopBuild()

    def __init__(self, cfg):
        self.cfg = cfg
        self.NS = cfg['NS']
        self.T = cfg['T']
        self.TS = cfg['TS']
        self.PA, self.PB, self.PC = cfg['PA'], cfg['PB'], cfg['PC']
        self.KEEP = min(512, self.T)
        self.nc = bass.Bass("TRN2", target_bir_lowering=False)
        self.P = Prog()
        self.ctx = ExitStack()
        self.rp = {}
        self.uid = 0

    def dram_in(self, name, shape):
        return self.nc.dram_tensor(name, list(shape), F32, kind="ExternalInput").ap()

    def dram_out(self, name, shape):
        return self.nc.dram_tensor(name, list(shape), F32, kind="ExternalOutput").ap()

    def sb(self, name, shape, dt):
        return self.ctx.enter_context(self.nc.sbuf_tensor(name, list(shape), dt))

    def bank(self, grp):
        ring = self.rings[grp]
        i = ring[self.rp.get(grp, 0) % len(ring)]
        self.rp[grp] = self.rp.get(grp, 0) + 1
        return i

    def mm(self, out, lhsT, rhs, st, sp, r, w):
        self.P.add('pe', lambda e: e.matmul(out, lhsT=lhsT, rhs=rhs, start=st, stop=sp), r=r, w=w)

    def tr(self, out, in_, ident, r, w):
        self.P.add('pe', lambda e: e.transpose(out, in_, ident), r=r, w=w)

    def act(self, out, in_, func, r, w, bias=0.0, scale=1.0, accum=None):
        if accum is None:
            self.P.add('act', lambda e: e.activation(out=out, in_=in_, func=func, bias=bias, scale=scale), r=r, w=w)
        else:
            self.P.add('act', lambda e: e.activation(out=out, in_=in_, func=func, bias=bias, scale=scale,
                                                     accum_out=accum), r=r, w=w)

    def copy(self, eng, out, in_, r, w):
        if eng == 'act':
            self.P.add('act', lambda e: e.copy(out=out, in_=in_), r=r, w=w)
        else:
            self.P.add(eng, lambda e: e.tensor_copy(out=out, in_=in_), r=r, w=w)

    def tt(self, eng, out, in0, in1, op, r, w):
        self.P.add(eng, lambda e: e.tensor_tensor(out=out, in0=in0, in1=in1, op=op), r=r, w=w)

    def ts(self, eng, out, in0, s1, s2, op0, op1, r, w):
        if op1 is None:
            self.P.add(eng, lambda e: e.tensor_scalar(out=out, in0=in0, scalar1=s1, scalar2=None, op0=op0), r=r, w=w)
        else:
            self.P.add(eng, lambda e: e.tensor_scalar(out=out, in0=in0, scalar1=s1, scalar2=s2, op0=op0, op1=op1),
                       r=r, w=w)

    def stt(self, eng, out, in0, scalar, in1, op0, op1, r, w):
        self.P.add(eng, lambda e: e.scalar_tensor_tensor(out=out, in0=in0, scalar=scalar, in1=in1, op0=op0, op1=op1),
                   r=r, w=w)

    def dma(self, q, out, in_, r, w, nobar=False):
        self.P.add(q, lambda e: e.dma_start(out=out, in_=in_), r=r, w=w, dma=True, nobar=nobar)

    def memset(self, eng, ap, val, w):
        self.P.add(eng, lambda e: e.memset(ap, val), r=(), w=w)

    def declare(self):
        NS, T, TS = self.NS, self.T, self.TS
        PA, PB, PC, KEEP = self.PA, self.PB, self.PC, self.KEEP
        di, do = self.dram_in, self.dram_out
        self.xp = di("xp", [NS, T, D])
        self.xs = di("xs", [TS, D])
        self.mem = di("mem", [NS, NMEM, D])
        self.ca_k = di("ca_k", [PA, 512]); self.ca_v = di("ca_v", [PA, 512])
        self.cb_k = di("cb_k", [PB, 512]); self.cb_v = di("cb_v", [PB, 512])
        self.cc_k = di("cc_k", [PC, 1024]); self.cc_v = di("cc_v", [PC, 1024])
        self.cc_lf = di("cc_lf", [PC, 16])
        self.cm_k = di("cm_k", [2, NMEM, D]); self.cm_v = di("cm_v", [2, NMEM, D])
        self.w_in_ab = di("w_in_ab", [D, 3072]); self.w_out_ab = di("w_out_ab", [D, D])
        self.relext = di("relext", [8, LEXT])
        self.w_in_c = di("w_in_c", [D, 3088]); self.b_f = di("b_f", [16]); self.w_out_c = di("w_out_c", [D, D])
        self.gvec = di("gvec", [128, 72])
        self.w_xq = di("w_xq", [2, D, D]); self.w_xk = di("w_xk", [2, D, D])
        self.w_xv = di("w_xv", [2, D, D]); self.w_xo = di("w_xo", [2, D, D])
        self.w_gate = di("w_gate", [2, D, DFF]); self.w_up = di("w_up", [2, D, DFF])
        self.w_down = di("w_down", [2, DFF, D])
        self.cf32 = di("cf32", [128, 384])
        self.cb16 = di("cb16", [128, 1410 + 896])
        self.csel = di("csel", [128, 16 * 128])
        self.y_p = do("y_p", [NS, T, D]); self.y_s = do("y_s", [TS, D])
        self.a_k_p = do("a_k_p", [NS, KEEP, 512]); self.a_v_p = do("a_v_p", [NS, KEEP, 512])
        self.b_k_p = do("b_k_p", [NS, T, 512]); self.b_v_p = do("b_v_p", [NS, T, 512])
        self.c_k_p = do("c_k_p", [NS, T, 1024]); self.c_v_p = do("c_v_p", [NS, T, 1024])
        self.c_lf_p = do("c_lf_p", [NS, T, 16])
        self.mem_k_p = do("mem_k_p", [2, NS, NMEM, D]); self.mem_v_p = do("mem_v_p", [2, NS, NMEM, D])
        self.a_k_s = do("a_k_s", [TS, 512]); self.a_v_s = do("a_v_s", [TS, 512])
        self.b_k_s = do("b_k_s", [TS, 512]); self.b_v_s = do("b_v_s", [TS, 512])
        self.c_k_s = do("c_k_s", [TS, 1024]); self.c_v_s = do("c_v_s", [TS, 1024])
        self.c_lf_s = do("c_lf_s", [TS, 16])

        sb = self.sb
        TM = max(T, TS)
        self.TM = TM
        NKT = max(T // 128, max(PB, PC) // 128 + 1, 2)
        self.NKT = NKT
        KTC = max(T, max(PB, PC) + TS)
        self.xT = sb("xT", [128, NCH, TM], F32)
        self.hT = sb("hT", [128, NCH, TM], BF16)
        self.oT = sb("oT", [128, NCH, TM], BF16)
        QN, KN, VN = TM, KTC, NKT * 192
        XT = min(512, TM)
        RN = max(QN + KN + VN, 2 * XT + 3 * 2048)
        self.R = sb("R", [128, RN], BF16)
        self.qT = self.R[:, 0:QN]
        self.kT = self.R[:, QN:QN + KN]
        self.vS = self.R[:, QN + KN:QN + KN + VN].rearrange("p (k n) -> p k n", n=192)
        self.q2 = self.R[:, 0:2 * XT].rearrange("p (a t) -> p a t", a=2)
        self.mkT = self.R[:, 2 * XT:2 * XT + 2048].rearrange("p (c m) -> p c m", c=NCH)
        self.mv = self.R[:, 2 * XT + 2048:2 * XT + 4096].rearrange("p (a n) -> p a n", a=2)
        self.mhT = self.R[:, 2 * XT + 4096:2 * XT + 6144].rearrange("p (c m) -> p c m", c=NCH)
        self.wsl = [sb(f"wsl{i}", [128, 4096], BF16) for i in range(2)]
        self.wrp = 0
        self.biasA = sb("biasA", [128, max(8 * 4 * 128, TM)], BF16)
        self.cumT8 = self.biasA[:, 0:TM]
        self.cB = sb("cB", [128, 1410 + 896], BF16)
        self.cF = sb("cF", [128, 384], F32)
        self.sel = sb("sel", [128, 16 * 128], BF16)
        self.gv = sb("gv", [128, 72], F32)
        self.bfb = sb("bfb", [128, 16], F32)
        self.xst = sb("xst", [128, D], F32)
        self.toe = self.xst[:, :].rearrange("p (h q) -> p h q", h=8)
        self.kvst = [sb(f"kvst{i}", [128, 384], F32) for i in range(2)]
        self.kvrp = 0
        self.tmpf = [sb(f"tmpf{i}", [128, 512], F32) for i in range(2)]
        self.spb = [sb(f"spb{i}", [128, 512], BF16) for i in range(4)]
        self.sprp = 0
        self.tfrp = 0
        self.wtb = [sb(f"wtb{i}", [128, 512], BF16) for i in range(4)]
        self.wtrp = 0
        self.racc = [sb(f"racc{i}", [128, 512], BF16) for i in range(4)]
        self.sqr = [sb(f"sq{i}", [128, 512], BF16) for i in range(2)]
        self.rden = sb("rden", [128, 512], F32)
        self.rstd = self.rden
        self.negcum = sb("negcum", [128, NKT, 16], F32)
        self.lfs = sb("lfs", [128, NKT, 16], F32)
        self.sm = sb("sm", [128, 160], F32)
        self.ssq = sb("ssq", [128, 2], F32)
        self.FT = min(512, TM)
        if NCH * TM >= NFC * self.FT:
            self.aT = self.oT[:, :, :].rearrange("p c t -> p (c t)")[:, 0:NFC * self.FT].rearrange(
                "p (f t) -> p f t", f=NFC)
        else:
            self.aT = sb("aT", [128, NFC, self.FT], BF16)
        self.ps = [self.ctx.enter_context(self.nc.psum_tensor(f"ps{i}", [128, 512], F32)) for i in range(8)]
        self.rings = {'a': [0, 1, 2, 3], 'acc': [4, 5, 6, 7], 'm': [6, 7]}

    def ident_b(self, n=128):
        return self.cB[0:n, 0:n]

    def triu_m8(self):
        return self.cB[:, 128:256]

    def ones_m8(self):
        return self.cB[:, 256:384]

    def ones_b(self):
        return self.cB[:, 384:512]

    def maskw(self, d, le):
        off = (1410 if le else 512) + 384 - 128 * d
        return self.cB[:, off:off + 512]

    def ident_f(self, n=128):
        return self.cF[0:n, 0:n]

    def g(self, vi, c):
        return self.gv[:, vi * 8 + c: vi * 8 + c + 1]

    def setup(self):
        self.dma('pool', self.cB[:], self.cb16, r=(), w=['cB'])
        self.dma('sp', self.cF[:], self.cf32, r=(), w=['cF'])
        self.dma('pool', self.sel[:], self.csel, r=(), w=['sel'])
        self.dma('sp', self.gv[:], self.gvec, r=(), w=['gv'])
        bsrc = bass.AP(self.b_f.tensor, 0, [[0, 128], [1, 16]])
        self.dma('sp', self.bfb[:], bsrc, r=(), w=['bfb'])
        self.convert_units()
        self.P.barrier()

    JS = {0: 0, 1: 1, 2: 1, 3: 2, 4: 3}

    def bA(self):
        return self.biasA[:, 0:4096].rearrange("p (h j q) -> p h j q", h=8, j=4)

    def setup_bias(self):
        bA = self.bA()
        for j in (0, 1, 3, 4):
            src = bass.AP(self.relext.tensor, 128 * j, [[1, 128], [LEXT, 8], [1, 128]])
            self.dma('sp', self.toe[:], src, r=(), w=['xst0', 'xst1'])
            for h in range(8):
                t0 = self.toe[:, h, :]
                rev = bass.AP(t0.tensor, t0.offset + 127, [list(t0.ap[0]), [-1, 128]])
                self.ts('dve', bA[:, h, self.JS[j], :], rev, 8.0, None, ALU.mult, None, r=['xst0', 'xst1'], w=['biasA'])
        for h in range(8):
            self.memset('dve', bA[0:64, h, 0, 64:128], MASKV, w=['biasA'])
            self.memset('dve', bA[64:128, h, 3, 0:64], MASKV, w=['biasA'])

    def pipe(self, items, stages, lags):
        n = len(items)
        for step in range(n + max(lags)):
            for fn, lag in zip(stages, lags):
                k = step - lag
                if 0 <= k < n:
                    fn(items[k])

    def wslot(self):
        i = self.wrp % len(self.wsl)
        self.wrp += 1
        return i

    def unit_specs(self):
        U = {}

        def cols(w2d, col0, ncols, off, width):
            return (w2d[:, col0:col0 + ncols].rearrange("(c p) n -> p c n", p=128), 8, width, off, ncols)
        order = []

        def put(k, v):
            U[k] = v
            order.append(k)
        for hp in range(8):
            isA = hp < 4
            hq = hp if isA else hp - 4
            base = 0 if isA else 1536
            put(('inab', hp), (3072, [cols(self.w_in_ab, base + hq * 128, 128, 0, 384),
                                       cols(self.w_in_ab, base + 512 + hq * 128, 128, 128, 384),
                                       cols(self.w_in_ab, base + 1024 + hq * 128, 128, 256, 384)]))
        for half in range(2):
            put(('res', 'oab', half), (4096, [cols(self.w_out_ab, half * 512, 512, 0, 512)]))
        for l in range(2):
            if l == 1:
                put(('incf',), (128, [cols(self.w_in_c, 3072, 16, 0, 16)]))
                for hp in range(8):
                    put(('inc', hp), (3072, [cols(self.w_in_c, hp * 128, 128, 0, 384),
                                              cols(self.w_in_c, 1024 + hp * 128, 128, 128, 384),
                                              cols(self.w_in_c, 2048 + hp * 128, 128, 256, 384)]))
                for half in range(2):
                    put(('res', 'oc', half), (4096, [cols(self.w_out_c, half * 512, 512, 0, 512)]))
            for wh, w3 in enumerate((self.w_xk, self.w_xv)):
                for half in range(2):
                    put(('xkv', wh, l, half), (4096, [cols(w3[l], half * 512, 512, 0, 512)]))
            for h in range(4):
                put(('xq', l, h), (2048, [cols(self.w_xq[l], h * 256, 256, 0, 256)]))
            for half in range(2):
                put(('res', f'xo{l}', half), (4096, [cols(self.w_xo[l], half * 512, 512, 0, 512)]))
            for blk in range(11):
                put(('gu', l, blk), (4096, [cols(self.w_gate[l], blk * 256, 256, 0, 512),
                                             cols(self.w_up[l], blk * 256, 256, 256, 512)]))
            for fcb in range(6):
                nf = min(4, NFC - fcb * 4)
                src = self.w_down[l][fcb * 512:fcb * 512 + nf * 128, :].rearrange("(c p) n -> p c n", p=128)
                put(('dn', l, fcb), (nf * 1024, [(src, nf, 1024, 0, 1024)]))
        return U, order

    def convert_units(self):
        self.U, order = self.unit_specs()
        self.uscr = {}
        for i, k in enumerate(order):
            nelem, parts = self.U[k]
            scr = self.nc.dram_tensor(f"wu{i}", [128, nelem], BF16).ap()
            self.uscr[k] = scr
            for pi, (src, C, width, off, ncols) in enumerate(parts):
                dst = scr[:, 0:C * width].rearrange("p (c n) -> p c n", c=C)[:, :, off:off + ncols]
                self.dma('pool', dst, src, r=(), w=[('wu', k, pi)], nobar=True)

    def load_unit(self, key, slot, C, width):
        nelem, parts = self.U[key]
        self.dma('sp', self.wsl[slot][:, 0:nelem], self.uscr[key], r=[('wu', key, pi) for pi in range(len(parts))],
                 w=[f'ws{slot}'])
        return self.wsl[slot][:, 0:C * width].rearrange("p (c n) -> p c n", c=C)

    def load_x(self, src, Tq):
        TK = min(128, Tq)
        for i in range(Tq // TK):
            for half in range(2):
                xh = self.xst[:, half * 512:(half + 1) * 512]
                self.dma('sp', xh[0:TK, :], src[i * TK:(i + 1) * TK, half * 512:(half + 1) * 512], r=(),
                         w=[f'xst{half}'])
                b = self.bank('m')
                for cc in range(4):
                    self.tr(self.ps[b][:, cc * TK:(cc + 1) * TK], xh[0:TK, cc * 128:(cc + 1) * 128],
                            self.ident_f(TK), r=[f'xst{half}', 'cF'], w=[f'ps{b}'])
                dst = self.xT[:, half * 4:half * 4 + 4, i * TK:(i + 1) * TK]
                srcp = self.ps[b][:, 0:4 * TK].rearrange("p (c t) -> p c t", c=4)
                self.copy('act', dst, srcp, r=[f'ps{b}'],
                          w=[('xT', c, (i * TK) // 512) for c in range(half * 4, half * 4 + 4)])

    def norm(self, vi, Tq):
        TT = min(512, Tq)
        ns = self.cfg.get('nstop', 9)
        for t in range(Tq // TT):
            sl = slice(t * TT, (t + 1) * TT)
            b = self.bank('m')
            for c in range(NCH):
                k = c % 2
                self.act(self.sqr[k][:, 0:TT], self.xT[:, c, sl], AF.Square, r=[('xT', c, t)], w=[('sq', k)],
                         scale=1.0 / 32.0)
                if ns < 1:
                    continue
                self.mm(self.ps[b][:, 0:TT], self.ones_b(), self.sqr[k][:, 0:TT], c == 0, c == NCH - 1,
                        r=[('sq', k), 'cB'], w=[f'ps{b}'])
            if ns < 2:
                continue
            self.act(self.rstd[:, 0:TT], self.ps[b][:, 0:TT], AF.Ln, r=[f'ps{b}'], w=[('rden', 0), ('rden', 1)], bias=EPS)
            if ns < 3:
                continue
            self.act(self.rstd[:, 0:TT], self.rstd[:, 0:TT], AF.Exp, r=[('rden', 0), ('rden', 1)], w=[('rden', 0), ('rden', 1)], scale=-0.5)
            if ns < 4:
                continue
            for c in range(NCH):
                self.stt('dve', self.hT[:, c, sl], self.xT[:, c, sl], self.g(vi, c), self.rstd[:, 0:TT],
                         ALU.mult, ALU.mult, r=[('xT', c, t), ('rden', 0), ('rden', 1), 'gv'], w=[('hT', c, t)])

    def proj_fm(self, wv, woff, dst_fn, Tq, wkey, dkey_fn):
        TT = min(512, Tq)
        for t in range(Tq // TT):
            sl = slice(t * TT, (t + 1) * TT)
            b = self.bank('a')
            for c in range(NCH):
                self.mm(self.ps[b][:, 0:TT], wv[:, c, woff:woff + 128], self.hT[:, c, sl], c == 0, c == NCH - 1,
                        r=[wkey, ('hT', c, t)], w=[f'ps{b}'])
            self.copy('act', dst_fn(sl), self.ps[b][:, 0:TT], r=[f'ps{b}'], w=[dkey_fn(t)])

    def proj_res(self, uname, srcT, skey_fn, Tq):
        TT = min(512, Tq)
        for half in range(2):
            s = self.wslot()
            wv = self.load_unit(('res', uname, half), s, 8, 512)
            for nn in range(4):
                n = half * 4 + nn
                for t in range(Tq // TT):
                    sl = slice(t * TT, (t + 1) * TT)
                    b = self.bank('a')
                    for c in range(NCH):
                        self.mm(self.ps[b][:, 0:TT], wv[:, c, nn * 128:(nn + 1) * 128], srcT[:, c, sl],
                                c == 0, c == NCH - 1, r=[f'ws{s}', skey_fn(c, t)], w=[f'ps{b}'])
                    self.tt('dve', self.xT[:, n, sl], self.ps[b][:, 0:TT], self.xT[:, n, sl], ALU.add,
                            r=[f'ps{b}', ('xT', n, t)], w=[('xT', n, t)])

    def kv_tm(self, wv, koff, Tq, ktile0, kout, vout, hpcol, keep_from=0):
        TK = min(128, Tq)
        n = Tq // TK
        sts = {}

        def front(i):
            sl = slice(i * TK, (i + 1) * TK)
            t = (i * TK) // 512
            b = self.bank('a')
            for c in range(NCH):
                self.mm(self.ps[b][0:TK, 0:384], self.hT[:, c, sl], wv[:, c, 0:384], c == 0, c == NCH - 1,
                        r=[self.cur_wkey, ('hT', c, t)], w=[f'ps{b}'])
            si = self.kvrp % 4
            self.kvrp += 1
            sts[i] = si
            st, skey = self.kvring()[si]
            self.copy('dve', st[0:TK, :], self.ps[b][0:TK, 0:384], r=[f'ps{b}'], w=[skey])
            kt = ktile0 + i
            self.copy('pool', self.vS[0:TK, kt, 0:64], st[0:TK, 256:320], r=[skey], w=[('vS', kt)])
            self.copy('pool', self.vS[0:TK, kt, 128:192], st[0:TK, 320:384], r=[skey], w=[('vS', kt)])
            if i * TK >= keep_from:
                r0 = i * TK - keep_from
                self.dma('sp', kout[r0:r0 + TK, hpcol:hpcol + 128], st[0:TK, 128:256], r=[skey], w=())
                self.dma('sp', vout[r0:r0 + TK, hpcol:hpcol + 128], st[0:TK, 256:384], r=[skey], w=())

        def back(i):
            si = sts[i]
            st, skey = self.kvring()[si]
            kt = ktile0 + i
            t = (i * TK) // 512
            b2 = self.bank('m')
            self.tr(self.ps[b2][:, 0:TK], st[0:TK, 0:128], self.ident_f(TK), r=[skey, 'cF'], w=[f'ps{b2}'])
            self.tr(self.ps[b2][:, TK:2 * TK], st[0:TK, 128:256], self.ident_f(TK), r=[skey, 'cF'],
                    w=[f'ps{b2}'])
            c0 = self.kcol(kt)
            self.copy('act', self.qT[:, i * TK:(i + 1) * TK], self.ps[b2][:, 0:TK], r=[f'ps{b2}'], w=[('qT', t)])
            self.copy('act', self.kT[:, c0:c0 + TK], self.ps[b2][:, TK:2 * TK], r=[f'ps{b2}'], w=[('kT', kt)])

        self.pipe(list(range(n)), [front, back], [0, 1])

    def kvring(self):
        return [(self.kvst[0], 'kvst0'), (self.kvst[1], 'kvst1'),
                (self.tmpf[0][:, 0:384], 'tmpf0'), (self.tmpf[1][:, 0:384], 'tmpf1')]

    def kcol(self, kt):
        return kt * 128

    def load_cache_kv(self, ck, cv, npast, hpcol):
        for i in range(npast // 128):
            si = self.kvrp % 2
            self.kvrp += 1
            st = self.kvst[si]
            self.dma('sp', st[:, 0:128], ck[i * 128:(i + 1) * 128, hpcol:hpcol + 128], r=(), w=[f'kvst{si}'])
            b2 = self.bank('m')
            self.tr(self.ps[b2][:, 0:128], st[:, 0:128], self.ident_f(), r=[f'kvst{si}', 'cF'], w=[f'ps{b2}'])
            self.copy('act', self.kT[:, i * 128:(i + 1) * 128], self.ps[b2][:, 0:128], r=[f'ps{b2}'], w=[('kT', i)])
            self.dma('pool', self.vS[:, i, 0:64], cv[i * 128:(i + 1) * 128, hpcol:hpcol + 64], r=(), w=[('vS', i)])
            self.dma('pool', self.vS[:, i, 128:192], cv[i * 128:(i + 1) * 128, hpcol + 64:hpcol + 128], r=(),
                     w=[('vS', i)])

    def finish_head(self, bacc, hh, hp, sl, t, n):
        if hh == 0:
            num, den, dst = self.ps[bacc][0:64, 0:n], self.ps[bacc][64:128, 0:n], self.oT[0:64, hp, sl]
            rd = self.rden[0:64, 0:n]
        else:
            num, den, dst = self.ps[bacc][64:128, 0:n], self.ps[bacc][0:64, 0:n], self.oT[64:128, hp, sl]
            rd = self.rden[64:128, 0:n]
        if hh == 0:
            self.P.add('dve', lambda e: e.reciprocal(out=rd, in_=den), r=[f'ps{bacc}'], w=[('rden', hh)])
        else:
            self.act(rd, den, AF.Ln, r=[f'ps{bacc}'], w=[('rden', hh)])
            self.act(rd, rd, AF.Exp, r=[('rden', hh)], w=[('rden', hh)], scale=-1.0)
        self.tt('dve', dst, num, rd, ALU.mult, r=[f'ps{bacc}', ('rden', hh)], w=[('oT', hp, t), 'aT_all'])

    def vlhs(self, kt, hh, rows):
        return self.vS[0:rows, kt, hh * 64:hh * 64 + 128]

    def set_ones(self):
        self.memset('dve', self.vS[:, :, 64:128], 1.0, w=['vS_ones'])

    def attn_A(self, hp, Tq, npast):
        bA = self.bA()
        if npast == 0:
            tiles = [(pi * 128, 128, [(j, pi + j - 4, 128) for j in range(5) if pi + j - 4 >= 0])
                     for pi in range(Tq // 128)]
        else:
            tiles = [(0, Tq, [(j, j, 128) for j in range(4)] + [(4, 4, Tq)])]
        items = []
        for (q0, nq, blocks) in tiles:
            for hh in range(2):
                g = dict(q0=q0, nq=nq, hh=hh, nb=len(blocks))
                for bi, (j, kt, nk) in enumerate(blocks):
                    items.append(dict(g=g, bi=bi, j=j, kt=kt, nk=nk))

        def s0(it):
            g = it['g']
            hh, q0, nq, nk, kt = g['hh'], g['q0'], g['nq'], it['nk'], it['kt']
            h = hp * 2 + hh
            rows = slice(hh * 64, hh * 64 + 64)
            t = q0 // 512
            b = self.bank('a')
            c0 = self.kcol(kt)
            self.mm(self.ps[b][0:nk, 0:nq], self.kT[rows, c0:c0 + nk], self.qT[rows, q0:q0 + nq], True, False,
                    r=[('kT', kt), ('qT', t)], w=[f'ps{b}'])
            self.mm(self.ps[b][0:nk, 0:nq], self.ident_b(nk), bA[0:nk, h, self.JS[it['j']], 0:nq], False, True,
                    r=['cB', 'biasA'], w=[f'ps{b}'])
            wi = self.wtrp % 4
            self.wtrp += 1
            it['wi'] = wi
            self.act(self.wtb[wi][0:nk, 0:nq], self.ps[b][0:nk, 0:nq], AF.Exp, r=[f'ps{b}'], w=[f'wtb{wi}'],
                     scale=0.125)

        def s1(it):
            g = it['g']
            hh, q0, nq, nk, kt, wi = g['hh'], g['q0'], g['nq'], it['nk'], it['kt'], it['wi']
            if it['bi'] == 0:
                g['bacc'] = self.bank('acc')
            bacc = g['bacc']
            self.mm(self.ps[bacc][:, 0:nq], self.vlhs(kt, hh, nk), self.wtb[wi][0:nk, 0:nq],
                    it['bi'] == 0, it['bi'] == g['nb'] - 1, r=[('vS', kt), 'vS_ones', f'wtb{wi}'], w=[f'ps{bacc}'])
            if it['bi'] == g['nb'] - 1:
                self.finish_head(bacc, hh, hp, slice(q0, q0 + nq), q0 // 512, nq)

        self.pipe(items, [s0, s1], [0, 3])

    def attn_B(self, hp, Tq, npast):
        TT = min(512, Tq)
        npt = npast // 128
        streams = []
        for t in reversed(range(Tq // TT)):
            q0 = t * TT
            for hh in range(2):
                blocks = []
                if npast == 0:
                    for kt in range((q0 + TT) // 128 - 1, -1, -1):
                        d = kt - q0 // 128
                        blocks.append((kt, 128, d if d >= 0 else None))
                else:
                    blocks.append((npt, Tq, 0))
                    for kt in range(npt - 1, -1, -1):
                        blocks.append((kt, 128, None))
                streams.append(dict(t=t, hh=hh, q0=q0, blocks=blocks, nb=len(blocks)))
        NSL = 4
        pending = list(streams)
        active = [None] * NSL
        items = []
        while True:
            for sl in range(NSL):
                if active[sl] is None and pending:
                    st_ = pending.pop(0)
                    st_['slot'] = sl
                    st_['pos'] = 0
                    active[sl] = st_
            if all(a is None for a in active):
                break
            for sl in range(NSL):
                st_ = active[sl]
                if st_ is None:
                    continue
                kt, nk, d = st_['blocks'][st_['pos']]
                items.append(dict(s=st_, bi=st_['pos'], kt=kt, nk=nk, d=d))
                st_['pos'] += 1
                if st_['pos'] == st_['nb']:
                    active[sl] = None

        def s0(it):
            st_ = it['s']
            hh, q0, t, kt, nk, d = st_['hh'], st_['q0'], st_['t'], it['kt'], it['nk'], it['d']
            rows = slice(hh * 64, hh * 64 + 64)
            c0 = self.kcol(kt)
            b = self.bank('a')
            it['b'] = b
            x0 = 128 * d if (d is not None and npast == 0) else 0
            it['x0'] = x0
            pz = self.ps[b][0:nk, x0:TT]
            self.mm(pz, self.kT[rows, c0:c0 + nk], self.qT[rows, q0 + x0:q0 + TT], True, True,
                    r=[('kT', kt), ('qT', t)], w=[f'ps{b}'])
            ti = self.tfrp % 2
            self.tfrp += 1
            tf = self.tmpf[ti]
            self.act(tf[0:nk, x0:TT], pz, AF.Exp, r=[f'ps{b}'], w=[f'tmpf{ti}'], scale=0.125)
            si = self.sprp % 4
            self.sprp += 1
            it['si'] = si
            sp_ = self.spb[si]
            self.act(sp_[0:nk, x0:TT], tf[0:nk, x0:TT], AF.Ln, r=[f'tmpf{ti}'], w=[f'spb{si}'], bias=1.0)
            if d is not None:
                xm = min(x0 + 128, TT)
                self.tt('dve', sp_[0:nk, x0:xm], sp_[0:nk, x0:xm], self.maskw(d, False)[0:nk, x0:xm], ALU.mult,
                        r=[f'spb{si}', 'cB'], w=[f'spb{si}'])

        def s1(it):
            st_ = it['s']
            sl, bi, nb, nk, d, b, si, x0 = st_['slot'], it['bi'], st_['nb'], it['nk'], it['d'], it['b'], it['si'], it['x0']
            pz = self.ps[b][0:nk, x0:TT]
            sp_ = self.spb[si]
            ra = self.racc[sl]
            self.mm(pz, self.triu_m8()[0:nk, 0:nk], sp_[0:nk, x0:TT], False, bi == 0,
                    r=['cB', f'spb{si}'], w=[f'ps{b}'])
            if bi > 0:
                self.mm(pz, self.ones_m8()[0:st_['rrows'], 0:nk], ra[0:st_['rrows'], x0:TT], False, True,
                        r=['cB', f'racc{sl}'], w=[f'ps{b}'])
            if bi == 0:
                if nk < 128 or x0 > 0:
                    self.memset('pool', ra[:, 0:TT], 0.0, w=[f'racc{sl}'])
                self.copy('pool', ra[0:nk, x0:TT], sp_[0:nk, x0:TT], r=[f'spb{si}'], w=[f'racc{sl}'])
                st_['rrows'] = 128 if nb > 1 else nk
            elif bi < nb - 1:
                self.tt('pool', ra[0:nk, x0:TT], ra[0:nk, x0:TT], sp_[0:nk, x0:TT], ALU.add,
                        r=[f'racc{sl}', f'spb{si}'], w=[f'racc{sl}'])
            wi = self.wtrp % 4
            self.wtrp += 1
            it['wi'] = wi
            wt = self.wtb[wi]
            self.act(wt[0:nk, x0:TT], pz, AF.Exp, r=[f'ps{b}'], w=[f'wtb{wi}'], scale=0.125)
            if d is not None:
                xm = min(x0 + 128, TT)
                self.tt('dve', wt[0:nk, x0:xm], wt[0:nk, x0:xm], self.maskw(d, False)[0:nk, x0:xm], ALU.mult,
                        r=[f'wtb{wi}', 'cB'], w=[f'wtb{wi}'])

        def s2(it):
            st_ = it['s']
            sl, bi, nb, nk, kt, hh, q0, t = st_['slot'], it['bi'], st_['nb'], it['nk'], it['kt'], st_['hh'], st_['q0'], st_['t']
            x0, d = it['x0'], it['d']
            bacc = 4 + sl
            wt = self.wtb[it['wi']]
            rk = [('vS', kt), 'vS_ones', f"wtb{it['wi']}"]
            if bi == 0:
                if x0 > 0:
                    self.memset('dve', wt[0:nk, 0:x0], 0.0, w=[f"wtb{it['wi']}"])
                self.mm(self.ps[bacc][:, 0:TT], self.vlhs(kt, hh, nk), wt[0:nk, 0:TT], True, bi == nb - 1,
                        r=rk, w=[f"ps{bacc}"])
            else:
                self.mm(self.ps[bacc][:, x0:TT], self.vlhs(kt, hh, nk), wt[0:nk, x0:TT], False, bi == nb - 1,
                        r=rk, w=[f"ps{bacc}"])
            if bi == nb - 1:
                rs = slice(hh * 64, hh * 64 + 64)
                self.copy('dve', self.oT[rs, hp, q0:q0 + TT], self.ps[bacc][rs, 0:TT],
                          r=[f"ps{bacc}"], w=[('oT', hp, t), 'aT_all'])

        self.pipe(items, [s0, s1, s2], [0, 2, 4])

    def attn_C(self, hp, Tq, npast):
        TT = min(512, Tq)
        npt = npast // 128
        selv = self.sel[:].rearrange("p (h n) -> p h n", h=16)
        items = []
        for t in range(Tq // TT):
            q0 = t * TT
            for hh in range(2):
                if npast == 0:
                    blocks = [(kt, 128, (kt - q0 // 128) if kt >= q0 // 128 else None)
                              for kt in range((q0 + TT) // 128)]
                else:
                    blocks = [(kt, 128, None) for kt in range(npt)] + [(npt, Tq, 0)]
                g = dict(q0=q0, t=t, hh=hh, nb=len(blocks))
                for bi, (kt, nk, d) in enumerate(blocks):
                    items.append(dict(g=g, bi=bi, kt=kt, nk=nk, d=d))

        def s0(it):
            g = it['g']
            hh, q0, t, kt, nk, d = g['hh'], g['q0'], g['t'], it['kt'], it['nk'], it['d']
            h = hp * 2 + hh
            rows = slice(hh * 64, hh * 64 + 64)
            c0 = self.kcol(kt)
            b = self.bank('a')
            x0 = 128 * d if d is not None else 0
            it['x0'] = x0
            pz = self.ps[b][0:nk, x0:TT]
            self.mm(pz, self.kT[rows, c0:c0 + nk], self.qT[rows, q0 + x0:q0 + TT], True, False,
                    r=[('kT', kt), ('qT', t)], w=[f'ps{b}'])
            self.mm(pz, selv[:, h, 0:nk], self.cumT8[:, q0 + x0:q0 + TT], False, True,
                    r=['sel', ('cumT8', t)], w=[f'ps{b}'])
            wi = self.wtrp % 4
            self.wtrp += 1
            it['wi'] = wi
            wt = self.wtb[wi]
            self.act(wt[0:nk, x0:TT], pz, AF.Exp, r=[f'ps{b}', ('negcum', kt)], w=[f'wtb{wi}'],
                     bias=self.negcum[0:nk, kt, h:h + 1], scale=0.125)
            if d is not None:
                xm = min(x0 + 128, TT)
                self.tt('dve', wt[0:nk, x0:xm], wt[0:nk, x0:xm], self.maskw(d, True)[0:nk, x0:xm], ALU.mult,
                        r=[f'wtb{wi}', 'cB'], w=[f'wtb{wi}'])

        def s1(it):
            g = it['g']
            hh, q0, t, kt, nk, x0, wi = g['hh'], g['q0'], g['t'], it['kt'], it['nk'], it['x0'], it['wi']
            if it['bi'] == 0:
                g['bacc'] = self.bank('acc')
            bacc = g['bacc']
            self.mm(self.ps[bacc][:, x0:TT], self.vlhs(kt, hh, nk), self.wtb[wi][0:nk, x0:TT],
                    it['bi'] == 0, it['bi'] == g['nb'] - 1, r=[('vS', kt), 'vS_ones', f'wtb{wi}'], w=[f'ps{bacc}'])
            if it['bi'] == g['nb'] - 1:
                self.finish_head(bacc, hh, hp, slice(q0, q0 + TT), t, TT)

        self.pipe(items, [s0, s1], [0, 3])

    def gate_cum(self, wf_v, Tq, npast, lf_out):
        TK = min(128, Tq)
        npt = npast // 128
        ntile_new = Tq // TK
        for i in range(npt):
            self.dma('sp', self.lfs[:, i, :], self.cc_lf[i * 128:(i + 1) * 128, :], r=(), w=[('lfs', i)])
        for i in range(ntile_new):
            kt = npt + i
            sl = slice(i * TK, (i + 1) * TK)
            t = (i * TK) // 512
            b = self.bank('m')
            for c in range(NCH):
                self.mm(self.ps[b][0:TK, 0:16], self.hT[:, c, sl], wf_v[:, c, 0:16], c == 0, c == NCH - 1,
                        r=[self.cur_wkey, ('hT', c, t)], w=[f'ps{b}'])
            z = self.sm[0:TK, 0:16]
            self.tt('dve', z, self.ps[b][0:TK, 0:16], self.bfb[0:TK, :], ALU.add, r=[f'ps{b}', 'bfb'], w=['sm_z'])
            e_ = self.sm[0:TK, 16:32]
            self.act(e_, z, AF.Exp, r=['sm_z'], w=['sm_e'], scale=-1.0)
            l_ = self.sm[0:TK, 32:48]
            self.act(l_, e_, AF.Ln, r=['sm_e'], w=['sm_l'], bias=1.0)
            self.ts('dve', self.lfs[0:TK, kt, :], l_, -1.0, None, ALU.mult, None, r=['sm_l'], w=[('lfs', kt)])
            self.dma('sp', lf_out[i * TK:(i + 1) * TK, :], self.lfs[0:TK, kt, :], r=[('lfs', kt)], w=())
        nkt = npt + ntile_new
        self.memset('dve', self.cumT8[32:64, 0:Tq], 0.0, w=[('cumT8', t_) for t_ in range(max(1, Tq // 512))])
        self.memset('dve', self.cumT8[64:128, 0:Tq], 0.0, w=[('cumT8', t_) for t_ in range(max(1, Tq // 512))])
        for kt in range(nkt):
            nk = 128 if kt < npt else TK
            b = self.bank('m')
            pc = self.ps[b][0:nk, 0:16]
            for k2 in range(kt):
                n2 = 128 if k2 < npt else TK
                self.mm(pc, self.cF[0:n2, 256:256 + nk], self.lfs[0:n2, k2, :], k2 == 0, False,
                        r=['cF', ('lfs', k2)], w=[f'ps{b}'])
            self.mm(pc, self.cF[0:nk, 128:128 + nk], self.lfs[0:nk, kt, :], kt == 0, True,
                    r=['cF', ('lfs', kt)], w=[f'ps{b}'])
            self.ts('dve', self.negcum[0:nk, kt, :], pc, -1.0, None, ALU.mult, None, r=[f'ps{b}'], w=[('negcum', kt)])
            if kt >= npt:
                i = kt - npt
                t = (i * TK) // 512
                c8 = self.sm[0:TK, 48:64]
                self.ts('dve', c8, pc, 8.0, None, ALU.mult, None, r=[f'ps{b}'], w=['sm_c8'])
                spl = self.sm[0:TK, 64:112].rearrange("p (h j) -> p h j", j=3)
                tb = self.sm[0:TK, 128:144]
                self.copy('dve', self.b16tmp[0:TK, :], c8, r=['sm_c8'], w=['b16tmp'])
                self.copy('dve', spl[:, :, 0], self.b16tmp[0:TK, :], r=['b16tmp'], w=['sm_spl'])
                self.tt('dve', tb, c8, spl[:, :, 0], ALU.subtract, r=['sm_c8', 'sm_spl'], w=['sm_tb'])
                self.copy('dve', self.b16tmp[0:TK, :], tb, r=['sm_tb'], w=['b16tmp'])
                self.copy('dve', spl[:, :, 1], self.b16tmp[0:TK, :], r=['b16tmp'], w=['sm_spl'])
                self.tt('dve', tb, tb, spl[:, :, 1], ALU.subtract, r=['sm_tb', 'sm_spl'], w=['sm_tb'])
                self.copy('dve', self.b16tmp[0:TK, :], tb, r=['sm_tb'], w=['b16tmp'])
                self.copy('dve', spl[:, :, 2], self.b16tmp[0:TK, :], r=['b16tmp'], w=['sm_spl'])
                b2 = self.bank('m')
                self.tr(self.ps[b2][0:48, 0:TK], self.sm[0:TK, 64:112], self.ident_f(TK), r=['sm_spl', 'cF'],
                        w=[f'ps{b2}'])
                self.copy('dve', self.cumT8[0:48, i * TK:(i + 1) * TK], self.ps[b2][0:48, 0:TK], r=[f'ps{b2}'],
                          w=[('cumT8', t)])

    def mem_kv_prompt(self, layer, s):
        mhT = self.mhT
        for mt in range(2):
            self.dma('sp', self.xst[:, :], self.mem[s, mt * 128:(mt + 1) * 128, :], r=(), w=['xst0', 'xst1'])
            for hf in range(2):
                self.act(self.tmpf[hf][:, 0:512], self.xst[:, hf * 512:(hf + 1) * 512], AF.Square, r=['xst0', 'xst1'],
                         w=[f'tmpf{hf}'])
                self.P.add('dve', lambda e, hf=hf: e.reduce_sum(out=self.ssq[:, hf:hf + 1], in_=self.tmpf[hf][:, 0:512],
                                                                 axis=mybir.AxisListType.X),
                           r=[f'tmpf{hf}'], w=['ssq'])
            self.tt('dve', self.ssq[:, 0:1], self.ssq[:, 0:1], self.ssq[:, 1:2], ALU.add, r=['ssq'], w=['ssq'])
            self.act(self.ssq[:, 0:1], self.ssq[:, 0:1], AF.Ln, r=['ssq'], w=['ssq'], bias=EPS, scale=1.0 / D)
            self.act(self.ssq[:, 0:1], self.ssq[:, 0:1], AF.Exp, r=['ssq'], w=['ssq'], scale=-0.5)
            self.ts('dve', self.xst[:, :], self.xst[:, :], self.ssq[:, 0:1], None, ALU.mult, None,
                    r=['xst0', 'xst1', 'ssq'], w=['xst0', 'xst1'])
            for half in range(2):
                b = self.bank('m')
                for cc in range(4):
                    c = half * 4 + cc
                    self.tr(self.ps[b][:, cc * 128:(cc + 1) * 128], self.xst[:, c * 128:(c + 1) * 128],
                            self.ident_f(), r=['xst0', 'xst1', 'cF'], w=[f'ps{b}'])
                for cc in range(4):
                    c = half * 4 + cc
                    self.ts('dve', mhT[:, c, mt * 128:(mt + 1) * 128], self.ps[b][:, cc * 128:(cc + 1) * 128],
                            self.g(4 + layer, c), None, ALU.mult, None, r=[f'ps{b}', 'gv'], w=['mhT'])
        for which, (w3, outd) in enumerate(((self.w_xk, self.mem_k_p), (self.w_xv, self.mem_v_p))):
            for half in range(2):
                sl_ = self.wslot()
                wv = self.load_unit(('xkv', which, layer, half), sl_, 8, 512)
                for mt in range(2):
                    b = self.bank('a')
                    for c in range(NCH):
                        self.mm(self.ps[b][:, 0:512], mhT[:, c, mt * 128:(mt + 1) * 128], wv[:, c, :], c == 0,
                                c == NCH - 1, r=[f'ws{sl_}', 'mhT'], w=[f'ps{b}'])
                    st = self.tmpf[1]
                    self.copy('dve', st[:, 0:512], self.ps[b][:, 0:512], r=[f'ps{b}'], w=['tmpf1'])
                    self.dma('sp', outd[layer, s, mt * 128:(mt + 1) * 128, half * 512:(half + 1) * 512], st[:, 0:512],
                             r=['tmpf1'], w=())
                    if which == 0:
                        b2 = self.bank('m')
                        for cc in range(4):
                            self.tr(self.ps[b2][:, cc * 128:(cc + 1) * 128], st[:, cc * 128:(cc + 1) * 128],
                                    self.ident_f(), r=['tmpf1', 'cF'], w=[f'ps{b2}'])
                        dst = self.mkT[:, half * 4:half * 4 + 4, mt * 128:(mt + 1) * 128]
                        self.copy('act', dst, self.ps[b2][:, 0:512].rearrange("p (c t) -> p c t", c=4),
                                  r=[f'ps{b2}'], w=['mkT'])
                    else:
                        self.copy('act', self.mv[:, mt, half * 512:(half + 1) * 512], st[:, 0:512], r=['tmpf1'],
                                  w=['mv'])

    def mem_kv_sample(self, layer):
        for mt in range(2):
            self.dma('sp', self.xst[:, :], self.cm_k[layer, mt * 128:(mt + 1) * 128, :], r=(), w=['xst0', 'xst1'])
            for half in range(2):
                b = self.bank('m')
                for cc in range(4):
                    c = half * 4 + cc
                    self.tr(self.ps[b][:, cc * 128:(cc + 1) * 128], self.xst[:, c * 128:(c + 1) * 128],
                            self.ident_f(), r=['xst0', 'xst1', 'cF'], w=[f'ps{b}'])
                dst = self.mkT[:, half * 4:half * 4 + 4, mt * 128:(mt + 1) * 128]
                self.copy('act', dst, self.ps[b][:, 0:512].rearrange("p (c t) -> p c t", c=4), r=[f'ps{b}'],
                          w=['mkT'])
            self.dma('pool', self.mv[:, mt, :], self.cm_v[layer, mt * 128:(mt + 1) * 128, :], r=(), w=['mv'])

    def xattn(self, layer, Tq):
        TT = min(512, Tq)
        self.norm(2 + layer, Tq)
        for h in range(4):
            s_ = self.wslot()
            wv = self.load_unit(('xq', layer, h), s_, 8, 256)
            self.cur_wkey = f'ws{s_}'
            for t in range(Tq // TT):
                sl = slice(t * TT, (t + 1) * TT)
                for cc in range(2):
                    b = self.bank('a')
                    for c in range(NCH):
                        self.mm(self.ps[b][:, 0:TT], wv[:, c, cc * 128:(cc + 1) * 128], self.hT[:, c, sl], c == 0,
                                c == NCH - 1, r=[f'ws{s_}', ('hT', c, t)], w=[f'ps{b}'])
                    self.copy('act', self.q2[:, cc, 0:TT], self.ps[b][:, 0:TT], r=[f'ps{b}'], w=[('q2', cc)])
                wts = []
                for mt in range(2):
                    b = self.bank('a')
                    for cc in range(2):
                        self.mm(self.ps[b][:, 0:TT], self.mkT[:, 2 * h + cc, mt * 128:(mt + 1) * 128],
                                self.q2[:, cc, 0:TT], cc == 0, cc == 1, r=['mkT', ('q2', cc)], w=[f'ps{b}'])
                    wi = self.wtrp % 4
                    self.wtrp += 1
                    self.act(self.wtb[wi][:, 0:TT], self.ps[b][:, 0:TT], AF.Exp, r=[f'ps{b}'], w=[f'wtb{wi}'],
                             scale=1.0 / 16.0)
                    wts.append(wi)
                bd = self.bank('m')
                for mt in range(2):
                    self.mm(self.ps[bd][:, 0:TT], self.ones_b(), self.wtb[wts[mt]][:, 0:TT], mt == 0, mt == 1,
                            r=['cB', f'wtb{wts[mt]}'], w=[f'ps{bd}'])
                self.P.add('dve', lambda e, bd=bd: e.reciprocal(out=self.rden[:, 0:TT], in_=self.ps[bd][:, 0:TT]),
                           r=[f'ps{bd}'], w=[('rden', 0), ('rden', 1)])
                for cc in range(2):
                    bo = self.bank('acc')
                    for mt in range(2):
                        self.mm(self.ps[bo][:, 0:TT], self.mv[:, mt, (2 * h + cc) * 128:(2 * h + cc + 1) * 128],
                                self.wtb[wts[mt]][:, 0:TT], mt == 0, mt == 1, r=['mv', f'wtb{wts[mt]}'],
                                w=[f'ps{bo}'])
                    self.tt('dve', self.oT[:, 2 * h + cc, sl], self.ps[bo][:, 0:TT], self.rden[:, 0:TT], ALU.mult,
                            r=[f'ps{bo}', ('rden', 0), ('rden', 1)], w=[('oT', 2 * h + cc, t), 'aT_all'])
        self.proj_res(f'xo{layer}', self.oT, lambda c, t: ('oT', c, t), Tq)

    def ffn(self, layer, Tq):
        FT = min(self.FT, Tq)
        self.norm(6 + layer, Tq)
        TTn = min(512, Tq)
        blocks = [(c0_, 256) for c0_ in range(0, DFF, 256)]
        for ft in range(Tq // FT):
            sl = slice(ft * FT, (ft + 1) * FT)
            tkey = (ft * FT) // TTn
            for blk, (c0, ncol) in enumerate(blocks):
                sg_ = self.wslot()
                su_ = sg_
                wg = self.load_unit(('gu', layer, blk), sg_, 8, 512)
                for j in range(ncol // 128):
                    fc = c0 // 128 + j
                    bg = self.bank('a')
                    bu = self.bank('a')
                    for c in range(NCH):
                        self.mm(self.ps[bg][:, 0:FT], wg[:, c, j * 128:(j + 1) * 128], self.hT[:, c, sl], c == 0,
                                c == NCH - 1, r=[f'ws{sg_}', ('hT', c, tkey)], w=[f'ps{bg}'])
                    for c in range(NCH):
                        self.mm(self.ps[bu][:, 0:FT], wg[:, c, 256 + j * 128:256 + (j + 1) * 128], self.hT[:, c, sl],
                                c == 0, c == NCH - 1, r=[f'ws{su_}', ('hT', c, tkey)], w=[f'ps{bu}'])
                    k = fc % 2
                    self.act(self.tmpf[k][:, 0:FT], self.ps[bg][:, 0:FT], AF.Silu, r=[f'ps{bg}'], w=[f'tmpf{k}'])
                    self.tt('dve', self.aT[:, fc, 0:FT], self.ps[bu][:, 0:FT], self.tmpf[k][:, 0:FT], ALU.mult,
                            r=[f'ps{bu}', f'tmpf{k}'], w=[('aT', fc)])
            for fcb in range(6):
                nf = min(4, NFC - fcb * 4)
                sd_ = self.wslot()
                wd = self.load_unit(('dn', layer, fcb), sd_, nf, 1024)
                for n in range(NCH):
                    for j in range(nf):
                        fc = fcb * 4 + j
                        self.mm(self.ps[n][:, 0:FT], wd[:, j, n * 128:(n + 1) * 128], self.aT[:, fc, 0:FT], fc == 0,
                                fc == NFC - 1, r=[f'ws{sd_}', ('aT', fc), 'aT_all'], w=[f'ps{n}'])
            for n in range(NCH):
                self.tt('dve', self.xT[:, n, sl], self.ps[n][:, 0:FT], self.xT[:, n, sl], ALU.add,
                        r=[f'ps{n}', ('xT', n, tkey)], w=[('xT', n, tkey)])

    def final_out(self, Tq, ydst):
        TT = min(512, Tq)
        TK = min(128, Tq)
        yT = self.tmpf
        for t in range(Tq // TT):
            sl = slice(t * TT, (t + 1) * TT)
            b = self.bank('m')
            for c in range(NCH):
                k = c % 2
                self.act(self.sqr[k][:, 0:TT], self.xT[:, c, sl], AF.Square, r=[('xT', c, t)], w=[('sq', k)],
                         scale=1.0 / 32.0)
                self.mm(self.ps[b][:, 0:TT], self.ones_b(), self.sqr[k][:, 0:TT], c == 0, c == NCH - 1,
                        r=[('sq', k), 'cB'], w=[f'ps{b}'])
            self.act(self.rstd[:, 0:TT], self.ps[b][:, 0:TT], AF.Ln, r=[f'ps{b}'], w=[('rden', 0), ('rden', 1)], bias=EPS)
            self.act(self.rstd[:, 0:TT], self.rstd[:, 0:TT], AF.Exp, r=[('rden', 0), ('rden', 1)], w=[('rden', 0), ('rden', 1)], scale=-0.5)
            for i in range(TT // TK):
                tok = slice(t * TT + i * TK, t * TT + (i + 1) * TK)
                for half in range(2):
                    b2 = self.bank('a')
                    for cc in range(4):
                        c = half * 4 + cc
                        k = c % 2
                        self.stt('dve', yT[k][:, 0:TK], self.xT[:, c, tok], self.g(8, c),
                                 self.rstd[:, i * TK:(i + 1) * TK], ALU.mult, ALU.mult,
                                 r=[('xT', c, t), ('rden', 0), ('rden', 1), 'gv'], w=[f'tmpf{k}'])
                        self.tr(self.ps[b2][0:TK, cc * 128:(cc + 1) * 128], yT[k][:, 0:TK], self.ident_f(),
                                r=[f'tmpf{k}', 'cF'], w=[f'ps{b2}'])
                    self.copy('act', self.xst[0:TK, half * 512:(half + 1) * 512], self.ps[b2][0:TK, 0:512],
                              r=[f'ps{b2}'], w=[f'xst{half}'])
                    self.dma('sp', ydst[t * TT + i * TK:t * TT + (i + 1) * TK, half * 512:(half + 1) * 512],
                             self.xst[0:TK, half * 512:(half + 1) * 512], r=[f'xst{half}'], w=())

    def run_seq(self, kind, s):
        if kind == 'p':
            Tq, xsrc = self.T, self.xp[s]
            pa = pb = pc = 0
        else:
            Tq, xsrc = self.TS, self.xs
            pa, pb, pc = self.PA, self.PB, self.PC
        TK = min(128, Tq)
        self.load_x(xsrc, Tq)
        self.ck(1)
        self.setup_bias()
        self.ck(2)
        self.set_ones()
        self.norm(0, Tq)
        self.ck(3)
        for hp in range(8):
            self.ck(4 + hp)
            isA = hp < 4
            hq = hp if isA else hp - 4
            base = 0 if isA else 1536
            s_ = self.wslot()
            wv = self.load_unit(('inab', hp), s_, 8, 384)
            self.cur_wkey = f'ws{s_}'
            npast = pa if isA else pb
            if kind == 'p':
                if isA:
                    kout, vout, keep_from = self.a_k_p[s], self.a_v_p[s], Tq - self.KEEP
                else:
                    kout, vout, keep_from = self.b_k_p[s], self.b_v_p[s], 0
            else:
                kout, vout, keep_from = (self.a_k_s, self.a_v_s, 0) if isA else (self.b_k_s, self.b_v_s, 0)
                ck, cv = (self.ca_k, self.ca_v) if isA else (self.cb_k, self.cb_v)
                self.load_cache_kv(ck, cv, npast, hq * 128)
            self.kv_tm(wv, 128, Tq, npast // 128, kout, vout, hq * 128, keep_from)
            if isA:
                self.attn_A(hp, Tq, npast)
            else:
                self.attn_B(hp, Tq, npast)
        self.ck(12)
        self.proj_res('oab', self.oT, lambda c, t: ('oT', c, t), Tq)
        self.ck(13)
        self.layer_tail(0, kind, s, Tq)
        self.ck(20)
        self.P.barrier()
        self.set_ones()
        self.norm(1, Tq)
        s_ = self.wslot()
        wf = self.load_unit(('incf',), s_, 8, 16)
        self.cur_wkey = f'ws{s_}'
        self.gate_cum(wf, Tq, pc, self.c_lf_p[s] if kind == 'p' else self.c_lf_s)
        self.ck(21)
        for hp in range(8):
            self.ck(22 + hp)
            s_ = self.wslot()
            wv = self.load_unit(('inc', hp), s_, 8, 384)
            self.cur_wkey = f'ws{s_}'
            if kind == 'p':
                kout, vout = self.c_k_p[s], self.c_v_p[s]
            else:
                kout, vout = self.c_k_s, self.c_v_s
                self.load_cache_kv(self.cc_k, self.cc_v, pc, hp * 128)
            if self.cfg.get('cstop', 9) < 1:
                raise StopBuild()
            self.kv_tm(wv, 128, Tq, pc // 128, kout, vout, hp * 128, 0)
            if self.cfg.get('cstop', 9) < 2:
                raise StopBuild()
            self.attn_C(hp, Tq, pc)
        self.proj_res('oc', self.oT, lambda c, t: ('oT', c, t), Tq)
        self.layer_tail(1, kind, s, Tq)
        self.final_out(Tq, self.y_p[s] if kind == 'p' else self.y_s)

    def layer_tail(self, layer, kind, s, Tq):
        self.P.barrier()
        if kind == 'p':
            self.mem_kv_prompt(layer, s)
        else:
            self.mem_kv_sample(layer)
        self.ck(14 + 10 * layer)
        self.xattn(layer, Tq)
        self.ck(15 + 10 * layer)
        self.P.barrier()
        self.ffn(layer, Tq)
        self.ck(16 + 10 * layer)

    def build(self):
        with self.ctx:
            self.declare()
            self.b16tmp = self.sb("b16tmp", [128, 16], BF16)
            try:
                self.setup()
                self.ck(0)
                for s in range(self.NS):
                    self.run_seq('p', s)
                self.ck(100)
                self.run_seq('s', 0)
            except StopBuild:
                pass
            self.P.finish()
            self.P.emit(self.nc, self.ctx)
        return self.nc


def host_consts():
    import ml_dtypes
    cf32 = np.zeros((128, 384), np.float32)
    cf32[:, 0:128] = np.eye(128, dtype=np.float32)
    s = np.arange(128)[:, None]
    t = np.arange(128)[None, :]
    cf32[:, 128:256] = (s <= t).astype(np.float32)
    cf32[:, 256:384] = 1.0
    cb = np.zeros((128, 1410 + 896), np.float32)
    cb[:, 0:128] = np.eye(128)
    cb[:, 128:256] = -8.0 * (s >= t)
    cb[:, 256:384] = -8.0
    cb[:, 384:512] = 1.0
    u = np.arange(897)[None, :]
    cb[:, 512:1409] = (s < (u - 384)).astype(np.float32)
    u2 = np.arange(896)[None, :]
    cb[:, 1410:] = (s <= (u2 - 384)).astype(np.float32)
    sel = np.zeros((128, 16, 128), np.float32)
    for h in range(16):
        sel[3 * h:3 * h + 3, h, :] = 1.0
    return cf32, cb, sel.reshape(128, 16 * 128)


def relext_layout(rel):
    m = np.arange(LEXT)
    idx = np.clip(639 - m, -128, 128) + 128
    return np.ascontiguousarray(rel[idx, :].T)


_CACHE = {}


def get_nc(cfg):
    key = tuple(sorted(cfg.items()))
    if key not in _CACHE:
        _CACHE[key] = Builder(cfg).build()
    return _CACHE[key]


def make_in_maps(inp, cfg, ncores):
    NS = cfg['NS']
    cf32, cb, sel = host_consts()
    gnames = ['g_mix', 'g_xattn', 'g_mem', 'g_ffn']
    gv = np.zeros((128, 72), np.float32)
    vi = 0
    for nm in gnames:
        for l in range(2):
            gv[:, vi * 8:(vi + 1) * 8] = np.asarray(inp[nm][l]).reshape(8, 128).T
            vi += 1
    gv[:, 64:72] = np.asarray(inp['g_final']).reshape(8, 128).T
    f = lambda a: np.ascontiguousarray(np.asarray(a, dtype=np.float32))
    common = dict(
        w_in_ab=f(inp['w_in_ab'][0]), w_out_ab=f(inp['w_out_ab'][0]), relext=relext_layout(f(inp['rel_bias_a'][0])),
        w_in_c=f(inp['w_in_c'][0]), b_f=f(inp['b_f_c'][0]), w_out_c=f(inp['w_out_c'][0]), gvec=gv,
        w_xq=f(inp['w_xq']), w_xk=f(inp['w_xk']), w_xv=f(inp['w_xv']), w_xo=f(inp['w_xo']),
        w_gate=f(inp['w_gate']), w_up=f(inp['w_up']), w_down=f(inp['w_down']),
        cf32=cf32, cb16=cb, csel=sel)
    maps = []
    for i in range(ncores):
        m = dict(common)
        m['xp'] = f(inp['x_prompt'][i * NS:(i + 1) * NS])
        m['xs'] = f(inp['x_sample'][i])
        m['mem'] = f(inp['mem_prompt'][i * NS:(i + 1) * NS])
        m['ca_k'] = f(inp['cache_a_k'][0, i]).reshape(cfg['PA'], 512)
        m['ca_v'] = f(inp['cache_a_v'][0, i]).reshape(cfg['PA'], 512)
        m['cb_k'] = f(inp['cache_b_k'][0, i]).reshape(cfg['PB'], 512)
        m['cb_v'] = f(inp['cache_b_v'][0, i]).reshape(cfg['PB'], 512)
        m['cc_k'] = f(inp['cache_c_k'][0, i]).reshape(cfg['PC'], 1024)
        m['cc_v'] = f(inp['cache_c_v'][0, i]).reshape(cfg['PC'], 1024)
        m['cc_lf'] = f(inp['cache_c_logf'][0, i])
        m['cm_k'] = f(inp['cache_mem_k'][:, i]).reshape(2, NMEM, D)
        m['cm_v'] = f(inp['cache_mem_v'][:, i]).reshape(2, NMEM, D)
        maps.append(m)
    return maps


def gather(res, cfg, ncores):
    NS, T, TS = cfg['NS'], cfg['T'], cfg['TS']
    KEEP = min(512, T)
    R = res.results
    cat = lambda nm: np.concatenate([R[i][nm] for i in range(ncores)], axis=0)
    stk = lambda nm: np.stack([R[i][nm] for i in range(ncores)], axis=0)
    B = NS * ncores
    y_p = cat('y_p')
    y_s = stk('y_s')
    out = [y_p, y_s]
    out.append(cat('a_k_p').reshape(1, B, KEEP, 8, 64))
    out.append(cat('a_v_p').reshape(1, B, KEEP, 8, 64))
    out.append(cat('b_k_p').reshape(1, B, T, 8, 64))
    out.append(cat('b_v_p').reshape(1, B, T, 8, 64))
    out.append(cat('c_k_p').reshape(1, B, T, 16, 64))
    out.append(cat('c_v_p').reshape(1, B, T, 16, 64))
    out.append(cat('c_lf_p').reshape(1, B, T, 16))
    out.append(np.concatenate([R[i]['mem_k_p'] for i in range(ncores)], axis=1).reshape(2, B, NMEM, 4, 256))
    out.append(np.concatenate([R[i]['mem_v_p'] for i in range(ncores)], axis=1).reshape(2, B, NMEM, 4, 256))
    out.append(stk('a_k_s').reshape(1, ncores, TS, 8, 64))
    out.append(stk('a_v_s').reshape(1, ncores, TS, 8, 64))
    out.append(stk('b_k_s').reshape(1, ncores, TS, 8, 64))
    out.append(stk('b_v_s').reshape(1, ncores, TS, 8, 64))
    out.append(stk('c_k_s').reshape(1, ncores, TS, 16, 64))
    out.append(stk('c_v_s').reshape(1, ncores, TS, 16, 64))
    out.append(stk('c_lf_s').reshape(1, ncores, TS, 16))
    return tuple(np.ascontiguousarray(o, dtype=np.float32) for o in out)


def kernel(**inputs):
    ncores = 8
    cfg = dict(NS=4, T=2048, TS=32, PA=512, PB=1024, PC=1024)
    nc = get_nc(cfg)
    maps = make_in_maps(inputs, cfg, ncores)
    res = run_bass_kernel_spmd(nc, maps, core_ids=list(range(ncores)))
    return gather(res, cfg, ncores)
```
